# Optimizing a Trainium2 kernel written in Bass

```python
import math
import jax, jax.numpy as jnp
from jax import lax
import numpy as np

D_MODEL = 1024
BATCH = 8
SEQ = 2048
DEPTH = 1

HEAD_DIM = 64
N_HEADS = 8
N_KV_HEADS = 2
WINDOW = 128
BLOCK = 128
NUM_BUCKETS = 32
MAX_DISTANCE = 128
CONV_CH = 512
CONV_WIDTH = 31
D_FF = int(math.ceil(8 * D_MODEL / 3 / 256) * 256)
Q_COLS = N_HEADS * HEAD_DIM
KV_COLS = N_KV_HEADS * HEAD_DIM
CONV_COLS = 2 * CONV_CH
GATE_COLS = 2 * D_MODEL
IN_COLS = Q_COLS + 2 * KV_COLS + CONV_COLS + GATE_COLS
ALPHA = (2.0 * DEPTH) ** 0.25
BETA = (8.0 * DEPTH) ** -0.25
LN_EPS = 1e-5
NEG_INF = -1e30

kernel_name = "hybrid_gated_conformer_swa_encoder"


def layer_norm(x, g, b):
    xf = x.astype(jnp.float32)
    mu = jnp.mean(xf, axis=-1, keepdims=True)
    var = jnp.mean(jnp.square(xf - mu), axis=-1, keepdims=True)
    y = (xf - mu) * lax.rsqrt(var + LN_EPS) * g.astype(jnp.float32) + b.astype(jnp.float32)
    return y.astype(x.dtype)


def _t5_buckets(rel):
    nb = NUM_BUCKETS // 2
    ret = (rel > 0).astype(np.int32) * nb
    n = np.abs(rel)
    max_exact = nb // 2
    large = max_exact + (np.log(np.maximum(n, 1) / max_exact)
                         / np.log(MAX_DISTANCE / max_exact) * (nb - max_exact)).astype(np.int32)
    large = np.minimum(large, nb - 1)
    return (ret + np.where(n < max_exact, n, large)).astype(np.int32)


def windowed_gqa(q, k, v, rel_bias, sink):
    B, S, _ = q.shape
    nblk = S // BLOCK
    R = N_HEADS // N_KV_HEADS
    qb = q.reshape(B, nblk, BLOCK, N_KV_HEADS, R, HEAD_DIM)
    pad = ((0, 0), (BLOCK, BLOCK), (0, 0))
    kp = jnp.pad(k, pad).reshape(B, nblk + 2, BLOCK, N_KV_HEADS, HEAD_DIM)
    vp = jnp.pad(v, pad).reshape(B, nblk + 2, BLOCK, N_KV_HEADS, HEAD_DIM)
    kb = jnp.concatenate([kp[:, :-2], kp[:, 1:-1], kp[:, 2:]], axis=2)
    vb = jnp.concatenate([vp[:, :-2], vp[:, 1:-1], vp[:, 2:]], axis=2)

    qi = np.arange(BLOCK)[:, None]
    kj = np.arange(3 * BLOCK)[None, :]
    rel = kj - BLOCK - qi
    win = np.abs(rel) <= WINDOW
    kpos = (np.arange(nblk)[:, None] - 1) * BLOCK + np.arange(3 * BLOCK)[None, :]
    valid = (kpos >= 0) & (kpos < S)
    mask = jnp.asarray(win[None, :, :] & valid[:, None, :])
    buckets = jnp.asarray(_t5_buckets(rel))
    bias = jnp.transpose(rel_bias.astype(jnp.float32)[buckets], (2, 0, 1))
    bias = bias.reshape(N_KV_HEADS, R, BLOCK, 3 * BLOCK)

    scale = HEAD_DIM ** -0.5
    s = jnp.einsum('bnqgrd,bnkgd->bngrqk', qb.astype(jnp.float32), kb.astype(jnp.float32)) * scale
    s = s + bias[None, None]
    s = jnp.where(mask[None, :, None, None, :, :], s, NEG_INF)
    sink_col = jnp.broadcast_to(sink.astype(jnp.float32).reshape(1, 1, N_KV_HEADS, R, 1, 1),
                                s.shape[:-1] + (1,))
    p = jax.nn.softmax(jnp.concatenate([s, sink_col], axis=-1), axis=-1)[..., :-1]
    o = jnp.einsum('bngrqk,bnkgd->bnqgrd', p.astype(v.dtype), vb)
    return o.reshape(B, S, N_HEADS * HEAD_DIM)


def conformer_conv(u, conv_w, conv_b, ln_g, ln_b):
    glu = u[..., :CONV_CH] * jax.nn.sigmoid(u[..., CONV_CH:])
    dw = lax.conv_general_dilated(
        glu, conv_w[:, None, :].astype(glu.dtype), window_strides=(1,),
        padding=[((CONV_WIDTH - 1) // 2, (CONV_WIDTH - 1) // 2)],
        dimension_numbers=('NWC', 'WIO', 'NWC'), feature_group_count=CONV_CH)
    dw = dw + conv_b
    return jax.nn.silu(layer_norm(dw, ln_g, ln_b))


def setup_inputs(seed: int = 0) -> dict:
    key = jax.random.key(seed)
    ks = jax.random.split(key, 24)
    f32 = jnp.float32
    nrm = lambda k, shp, s: jax.random.normal(k, shp, f32) * s
    gain = lambda k, shp: 1.0 + nrm(k, shp, 0.02)
    w_in = nrm(ks[1], (DEPTH, D_MODEL, IN_COLS), D_MODEL ** -0.5)
    v0 = Q_COLS + KV_COLS
    w_in = w_in.at[:, :, v0:v0 + KV_COLS].multiply(BETA)
    return {
        "x": jax.random.normal(ks[0], (BATCH, SEQ, D_MODEL), f32),
        "ln_in_g": gain(ks[2], (D_MODEL,)),
        "ln_in_b": nrm(ks[3], (D_MODEL,), 0.02),
        "rel_bias": nrm(ks[4], (NUM_BUCKETS, N_HEADS), 0.1),
        "w_in": w_in,
        "b_gate": nrm(ks[5], (DEPTH, GATE_COLS), 0.02),
        "conv_w": nrm(ks[6], (DEPTH, CONV_WIDTH, CONV_CH), CONV_WIDTH ** -0.5),
        "conv_b": nrm(ks[7], (DEPTH, CONV_CH), 0.02),
        "conv_ln_g": gain(ks[8], (DEPTH, CONV_CH)),
        "conv_ln_b": nrm(ks[9], (DEPTH, CONV_CH), 0.02),
        "w_conv_out": nrm(ks[10], (DEPTH, CONV_CH, D_MODEL), BETA * CONV_CH ** -0.5),
        "w_attn_out": nrm(ks[11], (DEPTH, Q_COLS, D_MODEL), BETA * Q_COLS ** -0.5),
        "sink": nrm(ks[12], (DEPTH, N_HEADS), 0.5),
        "w_out": nrm(ks[13], (DEPTH, D_MODEL, D_MODEL), BETA * D_MODEL ** -0.5),
        "ln1_g": gain(ks[14], (DEPTH, D_MODEL)),
        "ln1_b": nrm(ks[15], (DEPTH, D_MODEL), 0.02),
        "w_gate": nrm(ks[16], (DEPTH, D_MODEL, D_FF), BETA * D_MODEL ** -0.5),
        "w_up": nrm(ks[17], (DEPTH, D_MODEL, D_FF), BETA * D_MODEL ** -0.5),
        "w_down": nrm(ks[18], (DEPTH, D_FF, D_MODEL), BETA * D_FF ** -0.5),
        "ln2_g": gain(ks[19], (DEPTH, D_MODEL)),
        "ln2_b": nrm(ks[20], (DEPTH, D_MODEL), 0.02),
    }


def reference(x, ln_in_g, ln_in_b, rel_bias, w_in, b_gate, conv_w, conv_b, conv_ln_g,
              conv_ln_b, w_conv_out, w_attn_out, sink, w_out, ln1_g, ln1_b,
              w_gate, w_up, w_down, ln2_g, ln2_b):
    x = layer_norm(x, ln_in_g, ln_in_b)
    o1 = Q_COLS
    o2 = o1 + KV_COLS
    o3 = o2 + KV_COLS
    o4 = o3 + CONV_COLS
    for l in range(DEPTH):
        u = jnp.einsum('bsd,dc->bsc', x, w_in[l])
        q, k, v = u[..., :o1], u[..., o1:o2], u[..., o2:o3]
        y_a = windowed_gqa(q, k, v, rel_bias, sink[l]) @ w_attn_out[l]
        y_c = conformer_conv(u[..., o3:o4], conv_w[l], conv_b[l],
                             conv_ln_g[l], conv_ln_b[l]) @ w_conv_out[l]
        gates = jax.nn.sigmoid(u[..., o4:] + b_gate[l])
        merged = gates[..., :D_MODEL] * y_a + gates[..., D_MODEL:] * y_c
        mix = merged @ w_out[l]
        x = layer_norm(ALPHA * x + mix, ln1_g[l], ln1_b[l])
        hid = jax.nn.silu(x @ w_gate[l]) * (x @ w_up[l])
        x = layer_norm(ALPHA * x + hid @ w_down[l], ln2_g[l], ln2_b[l])
    return x
```

```python
import numpy as np
import concourse.bass as bass
import concourse.mybir as mybir
from concourse.bass_utils import run_bass_kernel_spmd

F32 = mybir.dt.float32
BF16 = mybir.dt.bfloat16
AF = mybir.ActivationFunctionType
ALU = mybir.AluOpType

S = 2048
D = 1024
NT = S // 128
DFF = 2816
NFF = DFF // 128
ALPHA = 2.0 ** 0.25
EPS = 1e-5
SBUF_BASE = 16640
SBUF_TOP = 229312

C_BGATE = 0
C_CONVW = 16
C_CONVB = 140
C_CONVG = 144
C_CONVLB = 148
C_SINK = 152
C_SMALL = 160


class _Op:
    __slots__ = ("eng", "fn", "dma", "deps", "sig", "token", "extra")

    def __init__(self, eng, fn, dma):
        self.eng = eng
        self.fn = fn
        self.dma = dma
        self.deps = {}
        self.sig = False
        self.token = None
        self.extra = None


class Sched:
    ENGS = ("pe", "act", "dve", "pool", "sp")
    NSLOT = 8

    def __init__(self):
        self.ops = {e: [] for e in self.ENGS}
        self.all = []
        self.lastw = {}
        self.readers = {}
        self.dmas_since_fence = []
        self.tens = {}

    def register(self, name, off, nbytes):
        al = []
        for n, t in self.tens.items():
            if not (t["end"] <= off or off + nbytes <= t["off"]):
                al.append(n)
                t["aliases"].append(name)
        self.tens[name] = dict(off=off, end=off + nbytes, last={}, dmas=[], aliases=al)

    def _dep(self, op, o, kind):
        if o is op:
            return
        if o.eng == op.eng and not op.dma and not o.dma:
            if op.eng == "pe":
                return
            if kind != "raw":
                return
        op.deps[o] = True

    def add(self, eng, fn, reads=(), writes=(), dma=False):
        op = _Op(eng, fn, dma)
        for r in reads:
            w = self.lastw.get(r)
            if w is not None:
                self._dep(op, w, "raw")
        for r in writes:
            w = self.lastw.get(r)
            if w is not None:
                self._dep(op, w, "waw")
            for rd in self.readers.get(r, ()):
                self._dep(op, rd, "war")
        touched = set()
        for r in list(reads) + list(writes):
            tn = r[0] if isinstance(r, tuple) else r
            if tn in self.tens:
                touched.add(tn)
        for tn in touched:
            t = self.tens[tn]
            for a in t["aliases"]:
                ta = self.tens[a]
                for o in ta["last"].values():
                    self._dep(op, o, "war")
                for o in ta["dmas"]:
                    self._dep(op, o, "war")
            if dma:
                t["dmas"].append(op)
            else:
                t["last"][eng] = op
        for r in reads:
            self.readers.setdefault(r, []).append(op)
        for r in writes:
            self.lastw[r] = op
            self.readers[r] = []
        self.ops[eng].append(op)
        self.all.append(op)
        if dma:
            self.dmas_since_fence.append(op)
        return op

    def dma(self, eng, fn, reads=(), writes=()):
        return self.add(eng, fn, reads, writes, dma=True)

    def fence(self):
        lasts = []
        for e in self.ENGS:
            for o in reversed(self.ops[e]):
                if not o.dma and o.fn is not None:
                    lasts.append(o)
                    break
        dmas = list(self.dmas_since_fence)
        self.dmas_since_fence = []
        for e in self.ENGS:
            op = _Op(e, None, False)
            for o in lasts:
                if o.eng != e:
                    op.deps[o] = True
            for o in dmas:
                op.deps[o] = True
            self.ops[e].append(op)
            self.all.append(op)

    def finalize(self, nc, stack):
        for op in self.all:
            for d in op.deps:
                d.sig = True
        self.sems = {}
        for e in ("pe", "act", "dve", "pool"):
            self.sems[e] = stack.enter_context(nc.semaphore("s_" + e))
        self.slot_sems = {}
        for q in ("sp", "pool"):
            self.slot_sems[q] = [stack.enter_context(nc.semaphore("d_%s%d" % (q, i)))
                                 for i in range(self.NSLOT)]
        for e in self.ENGS:
            cnt = 0
            nd = 0
            for op in self.ops[e]:
                if op.dma:
                    slot = nd % self.NSLOT
                    k = nd // self.NSLOT + 1
                    sem = self.slot_sems[e][slot]
                    op.token = (sem, 16 * k)
                    op.extra = (sem, 16 * (k - 1)) if k > 1 else None
                    nd += 1
                elif op.sig:
                    cnt += 1
                    op.token = (self.sems[e], cnt)

    def run(self, ename, eng):
        seen = {}
        for op in self.ops[ename]:
            waits = {}
            for d in op.deps:
                sem, val = d.token
                key = id(sem)
                if key not in waits or waits[key][1] < val:
                    waits[key] = (sem, val)
            if op.extra is not None:
                sem, val = op.extra
                key = id(sem)
                if key not in waits or waits[key][1] < val:
                    waits[key] = (sem, val)
            for key, (sem, val) in waits.items():
                if seen.get(key, 0) < val:
                    eng.wait_ge(sem, val)
                    seen[key] = val
            if op.fn is None:
                continue
            ins = op.fn(eng)
            if op.dma:
                ins.then_inc(op.token[0], 16)
            elif op.sig:
                ins.then_inc(op.token[0], 1)


def build(debug=False):
    import contextlib
    nc = bass.Bass("TRN2", target_bir_lowering=False)

    def din(name, shape):
        return nc.dram_tensor(name, shape, F32, kind="ExternalInput").ap()

    x_d = din("x", [S, D])
    lnp_d = din("lnp", [128, 6 * D])
    win_d = din("win", [30, 128, 1024])
    small_d = din("small", [128, C_SMALL])
    bias_d = din("biasT", [3, 128, 1024])
    mask_d = din("mask", [3, 128, 1024])
    ident_d = din("ident", [128, 128])
    diag_d = din("diagw", [128, 4 * 31 * 128])
    wao_d = din("wao", [128, 4 * 1024])
    wco_d = din("wco", [128, 4 * 1024])
    wout_d = din("wout", [128, 8 * 1024])
    wg_d = din("wg", [NFF, 128, 1024])
    wu_d = din("wu", [NFF, 128, 1024])
    wd_d = din("wd", [128, NFF * 1024])
    out_d = nc.dram_tensor("out", [S, D], F32, kind="ExternalOutput").ap()
    dbg = {}
    if debug:
        for nm, shp in (("d_x0T", [128, 8 * S]), ("d_qT", [64, 8 * S]), ("d_kT", [64, 2 * S]),
                        ("d_v", [128, 16 * 128]), ("d_ycin", [128, 4 * S]), ("d_oT", [64, 8 * S]),
                        ("d_mT", [128, 8 * S]), ("d_hT", [128, 8 * S])):
            dbg[nm] = nc.dram_tensor(nm, shp, F32, kind="ExternalOutput").ap()

    sc = Sched()
    names = [0]

    def sb(off, shape, dt, name):
        names[0] += 1
        h = nc.alloc_sbuf_tensor_at("%s_%d" % (name, names[0]), shape, dt, offset=off)
        nbytes = int(np.prod(shape[1:])) * (4 if dt == F32 else 2)
        assert off >= SBUF_BASE and off + nbytes <= SBUF_TOP, (name, off, nbytes)
        sc.register(name, off, nbytes)
        return h, off + ((nbytes + 31) // 32) * 32

    def sbn(off, n, shape, dt, name):
        hs = []
        for i in range(n):
            h, off = sb(off, shape, dt, "%s%d" % (name, i))
            hs.append(h)
        return hs, off

    o = SBUF_BASE
    ident, o = sb(o, [128, 128], BF16, "ident")
    ones_bf, o = sb(o, [128, 64], BF16, "ones_bf")
    onesm, o = sb(o, [128, 128], F32, "onesm")
    small, o = sb(o, [128, C_SMALL], F32, "small")
    esink, o = sb(o, [128, 8], F32, "esink")
    epsc, o = sb(o, [128, 8], F32, "epsc")
    NST = 8
    stt = []
    for i in range(NST):
        st, o = sb(o, [128, 12], F32, "st%d" % i)
        mv, o = sb(o, [128, 2], F32, "mv%d" % i)
        rs, o = sb(o, [128, 1], F32, "rs%d" % i)
        nm, o = sb(o, [128, 1], F32, "nm%d" % i)
        stt.append((st, mv, rs, nm))
    mvA, o = sb(o, [128, 32], F32, "mvA")
    rsA, o = sb(o, [128, 16], F32, "rsA")
    nmA, o = sb(o, [128, 16], F32, "nmA")
    mvE, o = sbn(o, 2, [128, 8], F32, "mvE")
    rsE, o = sbn(o, 2, [128, 4], F32, "rsE")
    nmE, o = sbn(o, 2, [128, 4], F32, "nmE")
    lnp, o = sb(o, [128, 6 * D], F32, "lnp")
    MT_OFF = o
    mT, o = sb(MT_OFF, [128, 8 * S], BF16, "mT")
    AD_OFF = o
    x0T, o = sb(o, [128, 8 * S], BF16, "x0T")
    ycin, o = sb(o, [128, 4 * S], BF16, "ycin")
    oT, o = sb(o, [128, 8 * S], BF16, "oT")
    wbuf, o = sbn(o, 4, [128, 1024], BF16, "wbuf")
    PH_OFF = o

    ps = nc.alloc_psum_tensor("ps", [128, 4096], F32)
    psb = ps.bitcast(BF16)

    def bank(b):
        return ps[:, b * 512:(b + 1) * 512]

    def bankbf(b):
        return psb[:, b * 1024:(b + 1) * 1024]

    lnv = lnp[:, :].rearrange("p (r d) -> p r d", r=6)
    ln_ctr = [0]

    def ln_stages(src, reg, gi, bi, dst, dst_reg, save=None):
        k = ln_ctr[0] % NST
        ln_ctr[0] += 1
        st, mv, rs, nm = stt[k]
        rs, nm = rs[:, :], nm[:, :]
        sk, mk, rk, nk = "st%d" % k, "mv%d" % k, "rs%d" % k, "nm%d" % k
        if save is not None:
            rs, nm, rk, nk = save

        def L0():
            def f_stats(e):
                e.bn_stats(st[:, 0:6], src[:, 0:512])
                return e.bn_stats(st[:, 6:12], src[:, 512:1024])
            sc.add("dve", f_stats, reads=[reg], writes=[sk])
            sc.add("dve", lambda e: e.bn_aggr(mv[:, :], st[:, 0:12]), reads=[sk], writes=[mk])
            sc.add("act", lambda e: e.activation(rs, mv[:, 1:2], AF.Sqrt, bias=epsc[:, 0:1]),
                   reads=[mk, "epsc"], writes=[rk])

        def L1():
            sc.add("dve", lambda e: e.reciprocal(rs, rs), reads=[rk], writes=[rk])
            sc.add("dve", lambda e: e.scalar_tensor_tensor(nm, mv[:, 0:1], -1.0, rs, ALU.mult, ALU.mult),
                   reads=[mk, rk], writes=[nk])
            sc.add("act", lambda e: e.activation(src, src, AF.Identity, bias=nm, scale=rs),
                   reads=[reg, rk, nk], writes=[reg])
            sc.add("dve", lambda e: e.tensor_tensor(src, src, lnv[:, gi, :], ALU.mult),
                   reads=[reg, "lnp", "lnp2"], writes=[reg])

        def L2():
            sc.add("dve", lambda e: e.tensor_tensor(dst, src, lnv[:, bi, :], ALU.add),
                   reads=[reg, "lnp", "lnp2"], writes=[dst_reg])
        return [L0, L1, L2]

    def cast_load(dst_ap, src_ap, writes, reads=()):
        sc.dma("pool", lambda e: e.dma_start(out=dst_ap, in_=src_ap), reads=reads, writes=writes)

    def dbg_dump(name, src_handle, nparts, ncols):
        if not debug:
            return
        sc.fence()
        for c0 in range(0, ncols, 2048):
            c1 = min(ncols, c0 + 2048)
            sc.dma("pool", lambda e, c0=c0, c1=c1: e.dma_start(out=dbg[name][0:nparts, c0:c1],
                                                              in_=src_handle[0:nparts, c0:c1]))
        sc.fence()

    def load_consts():
        sc.dma("sp", lambda e: e.dma_start(out=small[:, :], in_=small_d), writes=["small"])
        sc.dma("sp", lambda e: e.dma_start(out=lnp[:, 0:2 * D], in_=lnp_d[:, 0:2 * D]), writes=["lnp"])

    def load_consts2():
        sc.dma("sp", lambda e: e.dma_start(out=lnp[:, 2 * D:4 * D], in_=lnp_d[:, 2 * D:4 * D]), writes=["lnp2"])
        sc.dma("sp", lambda e: e.dma_start(out=lnp[:, 4 * D:6 * D], in_=lnp_d[:, 4 * D:6 * D]), writes=["lnp2"])
    cast_load(ident[:, :], ident_d, ["ident"])
    sc.add("dve", lambda e: e.memset(ones_bf[:, :], 1.0), writes=["ones_bf"])
    sc.add("dve", lambda e: e.memset(onesm[:, :], 1.0 / 512.0), writes=["onesm"])
    sc.add("dve", lambda e: e.memset(epsc[:, :], EPS), writes=["epsc"])

    o = PH_OFF
    glu, o = sb(o, [128, 4 * 2080], BF16, "glu")
    sig, o = sbn(o, 2, [128, 512], F32, "sig")
    PH2 = o
    NXA = 8
    NXB = 4
    xtA, _ = sbn(AD_OFF + 32768 + 16384, NXA, [128, D], F32, "xtA")
    x0b, _ = sbn(AD_OFF + 32768, NXB, [128, D], BF16, "x0b")
    cwx, _ = sbn(AD_OFF + 32768 + 8192, 4, [128, 1024], BF16, "cw")
    cwt = [(wbuf[i], "wbuf%d" % i) for i in range(4)] + [(cwx[i], "cw%d" % i) for i in range(4)]
    x0Tv = x0T[:, :].rearrange("p (k s) -> p k s", k=8)
    o = MT_OFF
    diag, o = sb(o, [128, 4 * 31 * 128], BF16, "diag")
    assert o <= AD_OFF
    gluv = glu[:, :].rearrange("p (c s) -> p c s", c=4)
    diagv = diag[:, :].rearrange("p (c j m) -> p c j m", c=4, j=31)
    ycv = ycin[:, :].rearrange("p (c s) -> p c s", c=4)

    def transpose_to(srcb, src_reg, dstv, dst_reg, t, pb):
        def f_tr(e):
            ins = None
            for kc in range(8):
                ins = e.transpose(bankbf(pb)[:, kc * 128:(kc + 1) * 128],
                                  srcb[:, kc * 128:(kc + 1) * 128], ident[:, :])
            return ins
        sc.add("pe", f_tr, reads=[src_reg, "ident"], writes=[("ps", pb)])
        sc.add("act", lambda e: e.copy(dstv[:, :, t * 128:(t + 1) * 128],
                                       bankbf(pb)[:, 0:1024].rearrange("p (k c) -> p k c", k=8)),
               reads=[("ps", pb)], writes=[dst_reg])

    wctr = [0]

    def load_w(chunk):
        slot = wctr[0] % 4
        wctr[0] += 1
        cast_load(wbuf[slot][:, :], win_d[chunk], ["wbuf%d" % slot])
        return slot

    def wv(slot):
        return wbuf[slot][:, :].rearrange("p (k c) -> p k c", k=8)

    def x0T_regs(tc):
        return [("x0T", 4 * tc + i) for i in range(4)]

    def proj_w(wt, wname, c0, c1, tc, pb, prows):
        def f(e):
            ins = None
            w = wt[:, :].rearrange("p (k c) -> p k c", k=8)
            for kc in range(8):
                ins = e.matmul(bank(pb)[0:prows, :], w[:, kc, c0:c1],
                               x0Tv[:, kc, tc * 512:(tc + 1) * 512],
                               start=(kc == 0), stop=(kc == 7))
            return ins
        sc.add("pe", f, reads=[wname] + x0T_regs(tc), writes=[("ps", pb)])

    def proj(slot, c0, c1, tc, pb, prows):
        proj_w(wbuf[slot], "wbuf%d" % slot, c0, c1, tc, pb, prows)

    for cc in range(4):
        sc.add("pool", lambda e, cc=cc: e.memset(gluv[:, cc, 0:15], 0.0), writes=[("glu", cc)])
        sc.add("pool", lambda e, cc=cc: e.memset(gluv[:, cc, 15 + S:30 + S], 0.0), writes=[("glu", cc)])
    diag_todo = []
    for cc in range(4):
        for jj in range(0, 31, 8):
            j1 = min(31, jj + 8)
            c0, c1 = (cc * 31 + jj) * 128, (cc * 31 + j1) * 128
            diag_todo.append(lambda cc=cc, c0=c0, c1=c1: cast_load(
                diag[:, c0:c1], diag_d[:, c0:c1], [("diag", cc)]))

    def gl_ensure(p):
        pass

    def glu_block(tc, ccs=(0, 1, 2, 3)):
        for cc in ccs:
            p = tc * 4 + cc
            i2 = p % 2
            pa, pbk = 2 * i2, 2 * i2 + 1
            proj_w(cwt[cc][0], cwt[cc][1], 0, 128, tc, pa, 128)
            proj_w(cwt[4 + cc][0], cwt[4 + cc][1], 0, 128, tc, pbk, 128)
            sg = sig[i2]
            sc.add("act", lambda e, sg=sg, pbk=pbk: e.activation(sg[:, :], bank(pbk), AF.Sigmoid),
                   reads=[("ps", pbk)], writes=["sig%d" % i2])
            sc.add("dve", lambda e, sg=sg, pa=pa, cc=cc, tc=tc: e.tensor_tensor(
                gluv[:, cc, 15 + tc * 512:15 + (tc + 1) * 512], bank(pa), sg[:, :], ALU.mult),
                reads=[("ps", pa), "sig%d" % i2], writes=[("glu", cc)])

    mvAv = mvA[:, :].rearrange("p (t c) -> p t c", c=2)
    rsAv = rsA[:, :].rearrange("p (t o) -> p t o", o=1)
    nmAv = nmA[:, :].rearrange("p (t o) -> p t o", o=1)

    def a_stage(G):
        for i in range(4):
            t = 4 * G + i
            xi = t % NXA
            xs = xtA[xi]
            xn = "xtA%d" % xi
            k = ln_ctr[0] % NST
            ln_ctr[0] += 1
            st = stt[k][0]
            sk = "st%d" % k
            sc.dma("sp", lambda e, t=t, xs=xs: e.dma_start(out=xs[:, :], in_=x_d[t * 128:(t + 1) * 128, :]),
                   writes=[xn])

            def f_stats(e, st=st, xs=xs):
                e.bn_stats(st[:, 0:6], xs[:, 0:512])
                return e.bn_stats(st[:, 6:12], xs[:, 512:1024])
            sc.add("dve", f_stats, reads=[xn], writes=[sk])
            sc.add("dve", lambda e, st=st, t=t: e.bn_aggr(mvAv[:, t, :], st[:, 0:12]),
                   reads=[sk], writes=[("mvA", t)])

    def b_stage(G):
        tl = list(range(4 * G, 4 * G + 4))
        sc.add("act", lambda e: e.activation(rsAv[:, 4 * G:4 * G + 4, :], mvAv[:, 4 * G:4 * G + 4, 1:2],
                                             AF.Sqrt, bias=epsc[:, 0:1]),
               reads=[("mvA", t) for t in tl] + ["epsc"], writes=[("rsA", t) for t in tl])
        sc.add("dve", lambda e: e.reciprocal(rsA[:, 4 * G:4 * G + 4], rsA[:, 4 * G:4 * G + 4]),
               reads=[("rsA", t) for t in tl], writes=[("rsA", t) for t in tl])
        sc.add("dve", lambda e: e.scalar_tensor_tensor(
            nmAv[:, 4 * G:4 * G + 4, :], mvAv[:, 4 * G:4 * G + 4, 0:1], -1.0, rsAv[:, 4 * G:4 * G + 4, :],
            ALU.mult, ALU.mult),
            reads=[("mvA", t) for t in tl] + [("rsA", t) for t in tl], writes=[("nmA", t) for t in tl])

    def cF_stage(G, blk=None):
        for i in range(4):
            t = 4 * G + i
            xs = xtA[t % NXA]
            xn = "xtA%d" % (t % NXA)
            sc.add("act", lambda e, xs=xs, t=t: e.activation(xs[:, :], xs[:, :], AF.Identity,
                                                             bias=nmA[:, t:t + 1], scale=rsA[:, t:t + 1]),
                   reads=[xn, ("rsA", t), ("nmA", t)], writes=[xn])
        for i in range(4):
            t = 4 * G + i
            xs = xtA[t % NXA]
            xn = "xtA%d" % (t % NXA)
            xb = x0b[t % NXB]
            bn = "x0b%d" % (t % NXB)
            sc.add("dve", lambda e, xs=xs: e.tensor_tensor(xs[:, :], xs[:, :], lnv[:, 0, :], ALU.mult),
                   reads=[xn, "lnp"], writes=[xn])
            sc.add("dve", lambda e, xs=xs, xb=xb: e.tensor_tensor(xb[:, :], xs[:, :], lnv[:, 1, :], ALU.add),
                   reads=[xn, "lnp"], writes=[bn])
            if blk is not None:
                glu_block(blk, [i])

    def cB_stage(G):
        for i in range(4):
            t = 4 * G + i
            xb = x0b[t % NXB]
            bn = "x0b%d" % (t % NXB)
            transpose_to(xb, bn, x0Tv, ("x0T", t), t, 6 + t % 2)
        if G >= 1:
            for _ in range(4):
                if diag_todo:
                    diag_todo.pop(0)()

    a_stage(0)
    load_consts()
    b_stage(0)
    a_stage(1)
    for i in range(8):
        cast_load(cwt[i][0][:, :], win_d[6 + i], [cwt[i][1]], reads=["xtA%d" % (4 + i % 4)])
    cF_stage(0)
    cB_stage(0)
    b_stage(1)
    a_stage(2)
    cF_stage(1, 0)
    cB_stage(1)
    b_stage(2)
    a_stage(3)
    cF_stage(2, 1)
    cB_stage(2)
    b_stage(3)
    cF_stage(3, 2)
    cB_stage(3)
    glu_block(3)
    load_consts2()
    while diag_todo:
        diag_todo.pop(0)()
    dbg_dump("d_x0T", x0T, 128, 8 * S)

    o = PH2
    dwb, o = sbn(o, 2, [128, 4 * 512], F32, "dwb")
    sq, o = sb(o, [128, 4 * 512], F32, "sq")
    m2b, o = sb(o, [128, 512], F32, "m2b")
    varb, o = sb(o, [128, 512], F32, "varb")
    rstdb, o = sb(o, [128, 512], F32, "rstdb")
    tbuf, o = sbn(o, 2, [128, 512], F32, "tbuf")
    for tc in range(4):
        d = tc % 2
        dv = dwb[d][:, :].rearrange("p (c s) -> p c s", c=4)
        sqv = sq[:, :].rearrange("p (c s) -> p c s", c=4)
        dn = "dwb%d" % d
        for cc in range(4):
            pb = 4 + (cc % 2)

            def f_conv(e, cc=cc, tc=tc, pb=pb):
                ins = None
                for j in range(31):
                    ins = e.matmul(bank(pb), diagv[:, cc, j, :],
                                   gluv[:, cc, tc * 512 + j:tc * 512 + j + 512],
                                   start=(j == 0), stop=(j == 30))
                return ins
            sc.add("pe", f_conv, reads=[("glu", cc), ("diag", cc)], writes=[("ps", pb)])
            sc.add("act", lambda e, cc=cc, pb=pb, dv=dv: e.activation(
                dv[:, cc, :], bank(pb), AF.Identity, bias=small[:, C_CONVB + cc:C_CONVB + cc + 1]),
                reads=[("ps", pb), "small"], writes=[(dn, cc)])
            sc.add("act", lambda e, cc=cc, dv=dv, sqv=sqv: e.activation(sqv[:, cc, :], dv[:, cc, :], AF.Square),
                   reads=[(dn, cc)], writes=[("sq", cc)])

        def f_mean(e, dv=dv):
            ins = None
            for cc in range(4):
                ins = e.matmul(bank(6), onesm[:, :], dv[:, cc, :], start=(cc == 0), stop=(cc == 3))
            return ins
        sc.add("pe", f_mean, reads=[(dn, c) for c in range(4)] + ["onesm"], writes=[("ps", 6)])

        def f_ex2(e, sqv=sqv):
            ins = None
            for cc in range(4):
                ins = e.matmul(bank(7), onesm[:, :], sqv[:, cc, :], start=(cc == 0), stop=(cc == 3))
            return ins
        sc.add("pe", f_ex2, reads=[("sq", c) for c in range(4)] + ["onesm"], writes=[("ps", 7)])
        sc.add("act", lambda e: e.activation(m2b[:, :], bank(6), AF.Square), reads=[("ps", 6)], writes=["m2b"])
        sc.add("dve", lambda e: e.tensor_tensor(varb[:, :], bank(7), m2b[:, :], ALU.subtract),
               reads=[("ps", 7), "m2b"], writes=["varb"])
        sc.add("act", lambda e: e.activation(rstdb[:, :], varb[:, :], AF.Sqrt, bias=epsc[:, 0:1]),
               reads=["varb", "epsc"], writes=["rstdb"])
        sc.add("dve", lambda e: e.reciprocal(rstdb[:, :], rstdb[:, :]), reads=["rstdb"], writes=["rstdb"])
        for cc in range(4):
            tb = tbuf[cc % 2]
            tn = "tbuf%d" % (cc % 2)
            sc.add("dve", lambda e, cc=cc, tb=tb, dv=dv: e.tensor_tensor(tb[:, :], dv[:, cc, :], bank(6), ALU.subtract),
                   reads=[(dn, cc), ("ps", 6)], writes=[tn])
            sc.add("dve", lambda e, tb=tb: e.tensor_tensor(tb[:, :], tb[:, :], rstdb[:, :], ALU.mult),
                   reads=[tn, "rstdb"], writes=[tn])
            sc.add("act", lambda e, cc=cc, tc=tc, tb=tb: e.activation(
                ycv[:, cc, tc * 512:(tc + 1) * 512], tb[:, :], AF.Silu,
                bias=small[:, C_CONVLB + cc:C_CONVLB + cc + 1], scale=small[:, C_CONVG + cc:C_CONVG + cc + 1]),
                reads=[tn, "small"], writes=[("ycin", cc)])
    dbg_dump("d_ycin", ycin, 128, 4 * S)

    o = MT_OFF
    qT, o = sb(o, [128, 8 * S], BF16, "qT")
    assert o <= AD_OFF
    o = PH_OFF
    kT, o = sb(o, [128, 2 * S], BF16, "kT")
    vsb, o = sb(o, [128, 16 * 128], BF16, "v")
    expb, o = sb(o, [128, 3 * 1024], F32, "expb")
    bstage, o = sb(o, [128, 1024], F32, "bstage")
    mstage, o = sb(o, [128, 1024], F32, "mstage")
    esf, o = sb(o, [128, 1024], F32, "esf")
    NE = 3
    Eb, o = sbn(o, NE, [128, 512], F32, "E")
    NP = 9
    pTb, o = sbn(o, NP, [128, 512], BF16, "pT")
    trb, o = sbn(o, 2, [128, 512], F32, "tr")
    esh, o = sb(o, [128, 1024], BF16, "esh")
    esl, o = sb(o, [128, 1024], BF16, "esl")
    qTv = qT[:, :].rearrange("p (h s) -> p h s", h=8)
    qT4 = qT[:, :].rearrange("p (a b s) -> p a b s", a=2, b=4)
    oT5 = oT[:, :].rearrange("p (g i a s) -> p g a i s", g=2, i=2, a=2)
    oTce = oT[:, :].rearrange("p (c e s) -> p c e s", c=4, e=2)
    kTv = kT[:, :].rearrange("p (g s) -> p g s", g=2)
    vv = vsb[:, :].rearrange("p (t c) -> p t c", t=16)
    oTv = oT[:, :].rearrange("p (h s) -> p h s", h=8)
    esfv = esf[:, :].rearrange("p (h q) -> p h q", h=8)
    expbv = expb[:, :].rearrange("p (j c) -> p j c", j=3)

    sc.add("act", lambda e: e.activation(esink[0:64, :], small[0:64, C_SINK:C_SINK + 8], AF.Exp),
           reads=["small"], writes=["esink"])
    sc.add("pool", lambda e: e.memset(esf[0:64, :], 0.0), writes=["esf"])
    for h in range(8):
        sc.add("dve", lambda e, h=h: e.tensor_scalar(esfv[0:64, h, :], esfv[0:64, h, :], esink[0:64, h:h + 1],
                                                      None, ALU.add), reads=["esf", "esink"], writes=["esf"])
    sc.add("dve", lambda e: e.tensor_copy(esh[0:1, :], esf[0:1, :]), reads=["esf"], writes=["esh"])
    sc.add("dve", lambda e: e.tensor_tensor(esf[0:1, :], esf[0:1, :], esh[0:1, :], ALU.subtract),
           reads=["esf", "esh"], writes=["esf"])
    sc.add("dve", lambda e: e.tensor_copy(esl[0:1, :], esf[0:1, :]), reads=["esf"], writes=["esl"])
    for j in range(3):
        sc.dma("sp", lambda e, j=j: e.dma_start(out=bstage[:, :], in_=bias_d[j]), writes=["bstage"])
        sc.dma("sp", lambda e, j=j: e.dma_start(out=mstage[:, :], in_=mask_d[j]), writes=["mstage"])
        sc.add("act", lambda e: e.activation(bstage[:, :], bstage[:, :], AF.Exp), reads=["bstage"], writes=["bstage"])
        sc.add("dve", lambda e, j=j: e.tensor_tensor(expbv[:, j, :], bstage[:, :], mstage[:, :], ALU.mult),
               reads=["bstage", "mstage"], writes=["expb"])

    pctr = [0]

    def nextbank(lo, n):
        b = lo + pctr[0] % n
        pctr[0] += 1
        return b

    wslots = [load_w(c) for c in range(4)]
    sk_ = None
    for i in range(4):
        s_ = wslots[i]
        if i == 1:
            sk_ = load_w(4)
        if i == 2:
            sv_ = load_w(5)
        for tc in range(4):
            pb = nextbank(0, 4)
            proj(s_, 0, 128, tc, pb, 128)
            sc.add("act", lambda e, i=i, tc=tc, pb=pb: e.copy(qTv[:, i, tc * 512:(tc + 1) * 512], bank(pb)),
                   reads=[("ps", pb)], writes=[("qT", i)])
        sc.dma("sp", lambda e, i=i: e.dma_start(out=qTv[0:64, 4 + i, :], in_=qTv[64:128, i, :]),
               reads=[("qT", i)], writes=[("qT", 4 + i)])
    for tc in range(4):
        pb = nextbank(0, 4)
        proj(sk_, 0, 128, tc, pb, 128)
        sc.add("act", lambda e, tc=tc, pb=pb: e.copy(kTv[:, 0, tc * 512:(tc + 1) * 512], bank(pb)),
               reads=[("ps", pb)], writes=[("kT", 0)])
    sc.dma("sp", lambda e: e.dma_start(out=kTv[0:64, 1, :], in_=kTv[64:128, 0, :]),
           reads=[("kT", 0)], writes=[("kT", 1)])
    for tq in range(4):
        pb = nextbank(0, 4)

        def f_v(e, tq=tq, pb=pb):
            ins = None
            w = wv(sv_)
            for i in range(4):
                t = 4 * tq + i
                for kc in range(8):
                    ins = e.matmul(bank(pb)[:, i * 128:(i + 1) * 128], x0Tv[:, kc, t * 128:(t + 1) * 128],
                                   w[:, kc, :], start=(kc == 0), stop=(kc == 7))
            return ins
        sc.add("pe", f_v, reads=["wbuf%d" % sv_] + x0T_regs(tq), writes=[("ps", pb)])
        sc.add("act", lambda e, tq=tq, pb=pb: e.copy(
            vv[:, 4 * tq:4 * tq + 4, :], bank(pb).rearrange("p (t c) -> p t c", t=4)),
            reads=[("ps", pb)], writes=[("v", tq)])

    ectr = [0]
    pctr2 = [0]

    def att_front(n, g):
        jl = [j for j in range(3) if 0 <= n + j - 1 < 16]
        pts = []
        for j in jl:
            kb = n + j - 1
            pb = nextbank(0, 4)
            sc.add("pe", lambda e, g=g, kb=kb, n=n, pb=pb: e.matmul(
                bank(pb).rearrange("p (a b q) -> p a b q", a=2, b=2),
                kTv[0:64, g, kb * 128:(kb + 1) * 128],
                qT4[0:64, :, 2 * g:2 * g + 2, n * 128:(n + 1) * 128], start=True, stop=True),
                reads=[("kT", g)] + [("qT", sl) for sl in (2 * g, 2 * g + 1, 4 + 2 * g, 5 + 2 * g)],
                writes=[("ps", pb)])
            ei = ectr[0] % NE
            ectr[0] += 1
            Et = Eb[ei]
            sc.add("act", lambda e, Et=Et, pb=pb: e.activation(Et[:, :], bank(pb), AF.Exp, scale=0.125),
                   reads=[("ps", pb)], writes=["E%d" % ei])
            pi = pctr2[0] % NP
            pctr2[0] += 1
            pt = pTb[pi]
            mul_eng = "pool" if (pctr2[0] % 3 == 2) else "dve"
            sc.add(mul_eng, lambda e, Et=Et, pt=pt, j=j, g=g: e.tensor_tensor(
                pt[:, :], Et[:, :], expbv[:, j, g * 512:(g + 1) * 512], ALU.mult),
                reads=["E%d" % ei, "expb"], writes=["pT%d" % pi])
            pts.append((pi, pt, kb))
        return pts

    def att_back(n, g, pts):
        u = n * 2 + g
        ob = 4 + u % 2
        db = 6 + u % 2

        def f_pv(e):
            ins = None
            for idx, (pi, pt, kb) in enumerate(pts):
                ins = e.matmul(bank(ob)[0:64, :], vv[:, kb, g * 64:(g + 1) * 64], pt[:, :],
                               start=(idx == 0), stop=(idx == len(pts) - 1))
            return ins
        sc.add("pe", f_pv, reads=["pT%d" % p[0] for p in pts] + [("v", p[2] // 4) for p in pts],
               writes=[("ps", ob)])

        def f_den(e):
            for idx, (pi, pt, kb) in enumerate(pts):
                e.matmul(bank(db)[0:64, :], ones_bf[:, 0:64], pt[:, :], start=(idx == 0), stop=False)
            e.matmul(bank(db)[0:64, :], ones_bf[0:1, 0:64], esh[0:1, g * 512:(g + 1) * 512],
                     start=False, stop=False)
            return e.matmul(bank(db)[0:64, :], ones_bf[0:1, 0:64], esl[0:1, g * 512:(g + 1) * 512],
                            start=False, stop=True)
        sc.add("pe", f_den, reads=["pT%d" % p[0] for p in pts] + ["ones_bf", "esh", "esl"], writes=[("ps", db)])
        ti = u % 2
        tr = trb[ti]
        tn = "tr%d" % ti
        sc.add("act", lambda e: e.activation(tr[0:64, :], bank(db)[0:64, :], AF.Ln),
               reads=[("ps", db)], writes=[tn])
        sc.add("act", lambda e: e.activation(tr[0:64, :], tr[0:64, :], AF.Exp, scale=-1.0),
               reads=[tn], writes=[tn])
        sc.add("dve", lambda e: e.tensor_tensor(
            oT5[0:64, g, :, :, n * 128:(n + 1) * 128],
            bank(ob)[0:64, :].rearrange("p (a i q) -> p a i q", a=2, i=2),
            tr[0:64, :].rearrange("p (a i q) -> p a i q", a=2, i=2), ALU.mult),
            reads=[("ps", ob), tn], writes=[("oT", n)])
        if g == 1 and n % 4 == 3:
            tc = n // 4
            sc.dma("sp", lambda e, tc=tc: e.dma_start(out=oTce[64:128, :, 0, tc * 512:(tc + 1) * 512],
                                                       in_=oTce[0:64, :, 1, tc * 512:(tc + 1) * 512]),
                   reads=[("oT", 4 * tc + i) for i in range(4)], writes=[("oT2", tc)])

    units = [(n, g) for n in range(16) for g in range(2)]
    fr = {0: att_front(*units[0]), 1: att_front(*units[1])}
    for i, (n, g) in enumerate(units):
        if i + 2 < len(units):
            fr[i + 2] = att_front(*units[i + 2])
        att_back(n, g, fr.pop(i))
    dbg_dump("d_qT", qT, 64, 8 * S)
    dbg_dump("d_kT", kT, 64, 2 * S)
    dbg_dump("d_v", vsb, 128, 16 * 128)
    dbg_dump("d_oT", oT, 64, 8 * S)

    o = PH_OFF
    wao, o = sb(o, [128, 4 * 1024], BF16, "wao")
    wco, o = sb(o, [128, 4 * 1024], BF16, "wco")
    gab, o = sbn(o, 4, [128, 512], F32, "gab")
    t12, o = sbn(o, 4, [128, 512], F32, "t12")
    waov = wao[:, :].rearrange("p (k c) -> p k c", k=4)
    wcov = wco[:, :].rearrange("p (k c) -> p k c", k=4)
    mTv = mT[:, :].rearrange("p (k s) -> p k s", k=8)
    wout, _ = sb(SBUF_TOP - 16384, [128, 8 * 1024], BF16, "wout")
    woutv = wout[:, :].rearrange("p (k c) -> p k c", k=8)
    assert o <= SBUF_TOP - 16384
    gslots = {0: (load_w(14), load_w(22))}
    for i in range(2):
        cast_load(wao[:, i * 2048:(i + 1) * 2048], wao_d[:, i * 2048:(i + 1) * 2048], ["wao"])
    for i in range(2):
        cast_load(wco[:, i * 2048:(i + 1) * 2048], wco_d[:, i * 2048:(i + 1) * 2048], ["wco"])
    for f in range(8):
        if f + 1 < 8:
            gslots[f + 1] = (load_w(14 + f + 1), load_w(22 + f + 1))
        if 2 <= f <= 5:
            i = f - 2
            cast_load(wout[:, i * 2048:(i + 1) * 2048], wout_d[:, i * 2048:(i + 1) * 2048], ["wout"])
        sga, sgc = gslots[f]
        for tc in range(4):
            i2 = (f * 4 + tc) % 2
            b0, b1, b2, b3 = 4 * i2, 4 * i2 + 1, 4 * i2 + 2, 4 * i2 + 3
            proj(sga, 0, 128, tc, b0, 128)
            proj(sgc, 0, 128, tc, b1, 128)

            def f_ya(e, f=f, tc=tc, b2=b2):
                ins = None
                for c in range(4):
                    ins = e.matmul(bank(b2), waov[:, c, f * 128:(f + 1) * 128],
                                   oTv[:, 2 * c, tc * 512:(tc + 1) * 512], start=(c == 0), stop=(c == 3))
                return ins
            sc.add("pe", f_ya, reads=["wao", ("oT2", tc)] + [("oT", 4 * tc + i) for i in range(4)],
                   writes=[("ps", b2)])

            def f_yc(e, f=f, tc=tc, b3=b3):
                ins = None
                for c in range(4):
                    ins = e.matmul(bank(b3), wcov[:, c, f * 128:(f + 1) * 128],
                                   ycv[:, c, tc * 512:(tc + 1) * 512], start=(c == 0), stop=(c == 3))
                return ins
            sc.add("pe", f_yc, reads=["wco"] + [("ycin", c) for c in range(4)], writes=[("ps", b3)])
            ga, gc = gab[2 * i2], gab[2 * i2 + 1]
            t1, t2 = t12[2 * i2], t12[2 * i2 + 1]
            gan, gcn = "gab%d" % (2 * i2), "gab%d" % (2 * i2 + 1)
            t1n, t2n = "t12%d" % (2 * i2), "t12%d" % (2 * i2 + 1)
            sc.add("act", lambda e, ga=ga, b0=b0, f=f: e.activation(
                ga[:, :], bank(b0), AF.Sigmoid, bias=small[:, C_BGATE + f:C_BGATE + f + 1]),
                reads=[("ps", b0), "small"], writes=[gan])
            sc.add("act", lambda e, gc=gc, b1=b1, f=f: e.activation(
                gc[:, :], bank(b1), AF.Sigmoid, bias=small[:, C_BGATE + 8 + f:C_BGATE + 8 + f + 1]),
                reads=[("ps", b1), "small"], writes=[gcn])
            sc.add("dve", lambda e, t1=t1, ga=ga, b2=b2: e.tensor_tensor(t1[:, :], bank(b2), ga[:, :], ALU.mult),
                   reads=[("ps", b2), gan], writes=[t1n])
            sc.add("dve", lambda e, t2=t2, gc=gc, b3=b3: e.tensor_tensor(t2[:, :], bank(b3), gc[:, :], ALU.mult),
                   reads=[("ps", b3), gcn], writes=[t2n])
            sc.add("pool", lambda e, t1=t1, t2=t2, f=f, tc=tc: e.tensor_tensor(
                mTv[:, f, tc * 512:(tc + 1) * 512], t1[:, :], t2[:, :], ALU.add),
                reads=[t1n, t2n], writes=[("mT", 4 * tc + i) for i in range(4)])
    dbg_dump("d_mT", mT, 128, 8 * S)

    o = AD_OFF
    wdn, o = sb(o, [128, NFF * 1024], BF16, "wdn")
    hid, o = sb(o, [128, NFF * 512], BF16, "hid")
    NXE = 8
    xtE, o = sbn(o, NXE, [128, D], F32, "xtE")
    hbb, o = sbn(o, 2, [128, D], BF16, "hbb")
    NZ = 3
    zb, o = sbn(o, NZ, [128, D], F32, "zb")
    sgb, o = sbn(o, 2, [128, 512], F32, "sgb")
    NWE = 6
    wE, o = sbn(o, NWE, [128, 1024], BF16, "wE")
    assert o <= SBUF_TOP - 16384
    wdnv = wdn[:, :].rearrange("p (k c) -> p k c", k=NFF)
    hidv = hid[:, :].rearrange("p (k s) -> p k s", k=NFF)

    ectr2 = [0]
    wd_loaded = [0]

    gu_calls = [0]

    def load_gu(ffc):
        gu_calls[0] += 1
        s0 = ectr2[0] % NWE
        s1 = (ectr2[0] + 1) % NWE
        ectr2[0] += 2
        cast_load(wE[s0][:, :], wg_d[ffc], ["wE%d" % s0])
        cast_load(wE[s1][:, :], wu_d[ffc], ["wE%d" % s1])
        if wd_loaded[0] < 11 and gu_calls[0] % 2 == 0:
            i = wd_loaded[0]
            wd_loaded[0] += 1
            cast_load(wdn[:, i * 2048:(i + 1) * 2048], wd_d[:, i * 2048:(i + 1) * 2048], [("wdn", i)])
        return (s0, s1)

    def s5_schedule(ST):
        sched = {}
        mul_eng = "dve"

        def at(step, fn):
            sched.setdefault(step, []).append(fn)
        bi_ = ST % 2
        mv4, rs4, nm4 = mvE[bi_], rsE[bi_], nmE[bi_]
        mvn, rsn, nmn = "mvE%d" % bi_, "rsE%d" % bi_, "nmE%d" % bi_
        mv4v = mv4[:, :].rearrange("p (i t) -> p i t", t=2)
        rs4v = rs4[:, :].rearrange("p (i o) -> p i o", o=1)
        nm4v = nm4[:, :].rearrange("p (i o) -> p i o", o=1)
        for tl in range(4):
            t = 4 * ST + tl
            xi = t % NXE
            xs = xtE[xi]
            xn = "xtE%d" % xi
            hi = t % 2
            hb = hbb[hi]
            hn = "hbb%d" % hi
            k = ln_ctr[0] % NST
            ln_ctr[0] += 1
            st = stt[k][0]
            sk = "st%d" % k

            def s0(t=t, xs=xs, xn=xn):
                sc.dma("sp", lambda e: e.dma_start(out=xs[:, :], in_=x_d[t * 128:(t + 1) * 128, :]), writes=[xn])

            def s1(t=t, xs=xs, xn=xn):
                sc.add("act", lambda e: e.activation(xs[:, :], xs[:, :], AF.Identity,
                                                     bias=nmA[:, t:t + 1], scale=rsA[:, t:t + 1]),
                       reads=[xn, ("rsA", t), ("nmA", t)], writes=[xn])
                sc.add(mul_eng, lambda e: e.tensor_tensor(xs[:, :], xs[:, :], lnv[:, 0, :], ALU.mult),
                       reads=[xn, "lnp"], writes=[xn])
                for half in range(2):
                    pb = half

                    def f_mix(e, half=half, pb=pb):
                        ins = None
                        for kc in range(8):
                            ins = e.matmul(bank(pb), mTv[:, kc, t * 128:(t + 1) * 128],
                                           woutv[:, kc, half * 512:(half + 1) * 512],
                                           start=(kc == 0), stop=(kc == 7))
                        return ins
                    sc.add("pe", f_mix, reads=[("mT", t), "wout"], writes=[("ps", pb)])

            def s2(t=t, tl=tl, xs=xs, xn=xn, st=st, sk=sk):
                sc.add("dve", lambda e: e.tensor_tensor(xs[:, :], xs[:, :], lnv[:, 1, :], ALU.add),
                       reads=[xn, "lnp"], writes=[xn])
                for half in range(2):
                    pb = half
                    sc.add("dve", lambda e, half=half, pb=pb: e.scalar_tensor_tensor(
                        xs[:, half * 512:(half + 1) * 512], xs[:, half * 512:(half + 1) * 512], ALPHA,
                        bank(pb), ALU.mult, ALU.add), reads=[xn, ("ps", pb)], writes=[xn])

            def s2b(tl=tl, xs=xs, xn=xn, st=st, sk=sk):
                def f_stats(e):
                    e.bn_stats(st[:, 0:6], xs[:, 0:512])
                    return e.bn_stats(st[:, 6:12], xs[:, 512:1024])
                sc.add("dve", f_stats, reads=[xn], writes=[sk])
                sc.add("dve", lambda e: e.bn_aggr(mv4v[:, tl, :], st[:, 0:12]), reads=[sk], writes=[(mvn, tl)])
            at(2 * tl, s0)
            at(2 * tl + 1, s1)
            at(2 * tl + 2, s2)
            at(2 * tl + 3, s2b)

        def s_sqrt():
            sc.add("act", lambda e: e.activation(rs4v, mv4v[:, :, 1:2], AF.Sqrt, bias=epsc[:, 0:1]),
                   reads=[(mvn, i) for i in range(4)] + ["epsc"], writes=[rsn])

        def s_rn():
            sc.add("dve", lambda e: e.reciprocal(rs4[:, :], rs4[:, :]), reads=[rsn], writes=[rsn])
            sc.add("dve", lambda e: e.scalar_tensor_tensor(nm4v, mv4v[:, :, 0:1], -1.0, rs4v, ALU.mult, ALU.mult),
                   reads=[(mvn, i) for i in range(4)] + [rsn], writes=[nmn])
        at(10, s_sqrt)
        at(11, s_rn)
        for tl in range(4):
            t = 4 * ST + tl
            xi = t % NXE
            xs = xtE[xi]
            xn = "xtE%d" % xi
            hi = t % 2
            hb = hbb[hi]
            hn = "hbb%d" % hi

            def s3(tl=tl, xs=xs, xn=xn):
                sc.add("act", lambda e: e.activation(xs[:, :], xs[:, :], AF.Identity,
                                                     bias=nm4[:, tl:tl + 1], scale=rs4[:, tl:tl + 1]),
                       reads=[xn, rsn, nmn], writes=[xn])
                sc.add(mul_eng, lambda e: e.tensor_tensor(xs[:, :], xs[:, :], lnv[:, 2, :], ALU.mult),
                       reads=[xn, "lnp2"], writes=[xn])

            def s4(xs=xs, xn=xn, hb=hb, hn=hn):
                sc.add("dve", lambda e: e.tensor_tensor(xs[:, :], xs[:, :], lnv[:, 3, :], ALU.add),
                       reads=[xn, "lnp2"], writes=[xn])
                sc.add("act", lambda e: e.copy(hb[:, :], xs[:, :]), reads=[xn], writes=[hn])

            def s5(t=t, hb=hb, hn=hn, hi=hi):
                transpose_to(hb, hn, mTv, ("mT", t), t, 2 + hi)
            at(12 + tl, s3)
            at(13 + tl, s4)
            at(14 + tl, s5)
        return sched

    sched0 = s5_schedule(0)
    for step in sorted(sched0):
        for fn in sched0[step]:
            fn()

    zctr = [0]
    dn_deferred = []
    pend = {}
    gu_ctr = [0]

    def prefetch_gu(gidx):
        if gidx < 4 * NFF and gidx not in pend:
            pend[gidx] = load_gu(gidx % NFF)

    prefetch_gu(0)
    prefetch_gu(1)
    for ST in range(4):
        bgs = s5_schedule(ST + 1) if ST + 1 < 4 else {}
        for ffc in range(NFF):
            gidx = ST * NFF + ffc
            prefetch_gu(gidx + 2)
            s0, s1 = pend.pop(gidx)
            i2 = ffc % 2
            bg, bu = 4 + 2 * i2, 5 + 2 * i2
            for (slot, pb) in ((s0, bg), (s1, bu)):
                def f_gu(e, slot=slot, pb=pb, ST=ST):
                    ins = None
                    w = wE[slot][:, :].rearrange("p (k c) -> p k c", k=8)
                    for kc in range(8):
                        ins = e.matmul(bank(pb), w[:, kc, :], mTv[:, kc, ST * 512:(ST + 1) * 512],
                                       start=(kc == 0), stop=(kc == 7))
                    return ins
                sc.add("pe", f_gu, reads=["wE%d" % slot] + [("mT", 4 * ST + i) for i in range(4)],
                       writes=[("ps", pb)])
            sg = sgb[i2]
            sc.add("act", lambda e, sg=sg, bg=bg: e.activation(sg[:, :], bank(bg), AF.Silu),
                   reads=[("ps", bg)], writes=["sgb%d" % i2])
            sc.add("dve", lambda e, sg=sg, bu=bu, ffc=ffc: e.tensor_tensor(
                hidv[:, ffc, :], bank(bu), sg[:, :], ALU.mult),
                reads=[("ps", bu), "sgb%d" % i2], writes=[("hid", ffc)])
            if ffc == 1:
                while dn_deferred:
                    dn_deferred.pop(0)()
            step = ffc - 2
            for fn in bgs.get(step, ()):
                fn()
        zst = {}
        for it in range(4 + 1):
            if it < 4:
                tl = it
                t = 4 * ST + tl
                zi = zctr[0] % NZ
                zctr[0] += 1
                z = zb[zi]
                zn = "zb%d" % zi
                hs = xtE[t % NXE]
                hsn = "xtE%d" % (t % NXE)
                for half in range(2):
                    pb = 2 * (tl % 4) + half

                    def f_dn(e, tl=tl, half=half, pb=pb):
                        ins = None
                        for k in range(NFF):
                            ins = e.matmul(bank(pb), hidv[:, k, tl * 128:(tl + 1) * 128],
                                           wdnv[:, k, half * 512:(half + 1) * 512],
                                           start=(k == 0), stop=(k == NFF - 1))
                        return ins
                    sc.add("pe", f_dn, reads=[("hid", k) for k in range(NFF)] + [("wdn", i) for i in range(11)],
                           writes=[("ps", pb)])
                    sc.add("dve", lambda e, z=z, hs=hs, half=half, pb=pb: e.scalar_tensor_tensor(
                        z[:, half * 512:(half + 1) * 512], hs[:, half * 512:(half + 1) * 512], ALPHA,
                        bank(pb), ALU.mult, ALU.add), reads=[hsn, ("ps", pb)], writes=[zn])
                zst[tl] = (ln_stages(z[:, :], zn, 4, 5, z[:, :], zn), z, zn, t)
                zst[tl][0][0]()
            if 0 <= it - 1 < 4:
                def fin(tup=zst[it - 1]):
                    lst, z, zn, t = tup
                    lst[1]()
                    lst[2]()
                    sc.dma("sp", lambda e: e.dma_start(out=out_d[t * 128:(t + 1) * 128, :], in_=z[:, :]),
                           reads=[zn])
                if it - 1 == 3 and ST < 3:
                    dn_deferred.append(fin)
                else:
                    fin()
    sc.fence()
    if debug:
        dbg_dump("d_hT", mT, 128, 8 * S)

    with contextlib.ExitStack() as stack:
        sc.finalize(nc, stack)
        with nc.Block() as block:
            @block.tensor
            def _(e):
                sc.run("pe", e)

            @block.scalar
            def _(e):
                sc.run("act", e)

            @block.vector
            def _(e):
                sc.run("dve", e)

            @block.gpsimd
            def _(e):
                sc.run("pool", e)

            @block.sync
            def _(e):
                sc.run("sp", e)
    return nc


NUM_BUCKETS = 32
HEAD_PERM = [0, 2, 1, 3, 4, 6, 5, 7]
MAX_DISTANCE = 128


def _t5_buckets(rel):
    nb = NUM_BUCKETS // 2
    ret = (rel > 0).astype(np.int32) * nb
    n = np.abs(rel)
    max_exact = nb // 2
    large = max_exact + (np.log(np.maximum(n, 1) / max_exact)
                         / np.log(MAX_DISTANCE / max_exact) * (nb - max_exact)).astype(np.int32)
    large = np.minimum(large, nb - 1)
    return (ret + np.where(n < max_exact, n, large)).astype(np.int32)


def _kchunk(w, ncols_chunk):
    K, C = w.shape
    nk = K // 128
    nch = C // ncols_chunk
    a = w.reshape(nk, 128, nch, ncols_chunk).transpose(2, 1, 0, 3)
    return np.ascontiguousarray(a.reshape(nch, 128, nk * ncols_chunk))


def _krows(w):
    K, C = w.shape
    nk = K // 128
    return np.ascontiguousarray(w.reshape(nk, 128, C).transpose(1, 0, 2).reshape(128, nk * C))


def prepare(inputs):
    f = lambda a: np.asarray(a, dtype=np.float32)
    shared = {}
    rep = lambda v: np.broadcast_to(f(v).reshape(1, -1), (128, f(v).size))
    shared["lnp"] = np.ascontiguousarray(np.concatenate(
        [rep(inputs["ln_in_g"]), rep(inputs["ln_in_b"]), rep(inputs["ln1_g"][0]), rep(inputs["ln1_b"][0]),
         rep(inputs["ln2_g"][0]), rep(inputs["ln2_b"][0])], axis=1))
    shared["win"] = _kchunk(f(inputs["w_in"][0]), 128)
    small = np.zeros((128, C_SMALL), np.float32)
    small[:, C_BGATE:C_BGATE + 16] = f(inputs["b_gate"][0]).reshape(16, 128).T
    cw = f(inputs["conv_w"][0])
    small[:, C_CONVW:C_CONVW + 124] = cw.T.reshape(4, 128, 31).transpose(1, 0, 2).reshape(128, 124)
    small[:, C_CONVB:C_CONVB + 4] = f(inputs["conv_b"][0]).reshape(4, 128).T
    small[:, C_CONVG:C_CONVG + 4] = f(inputs["conv_ln_g"][0]).reshape(4, 128).T
    small[:, C_CONVLB:C_CONVLB + 4] = f(inputs["conv_ln_b"][0]).reshape(4, 128).T
    small[:, C_SINK:C_SINK + 8] = f(inputs["sink"][0])[HEAD_PERM].reshape(1, 8)
    shared["small"] = small
    dg = np.zeros((128, 4, 31, 128), np.float32)
    pidx = np.arange(128)
    dg[pidx, :, :, pidx] = cw.T.reshape(4, 128, 31).transpose(1, 0, 2)
    shared["diagw"] = np.ascontiguousarray(dg.reshape(128, 4 * 31 * 128))
    k = np.arange(128)[:, None]
    q = np.arange(128)[None, :]
    rb = f(inputs["rel_bias"])
    biasT = np.zeros((3, 128, 8, 128), np.float32)
    mask = np.zeros((3, 128, 8, 128), np.float32)
    for j in range(3):
        rel = j * 128 + k - 128 - q
        bk = _t5_buckets(rel)
        biasT[j] = rb[bk].transpose(0, 2, 1)[:, HEAD_PERM, :]
        mask[j] = (np.abs(rel) <= 128).astype(np.float32)[:, None, :]
    shared["biasT"] = np.ascontiguousarray(biasT.reshape(3, 128, 1024))
    shared["mask"] = np.ascontiguousarray(mask.reshape(3, 128, 1024))
    shared["ident"] = np.eye(128, dtype=np.float32)
    shared["wao"] = _krows(f(inputs["w_attn_out"][0]))
    shared["wco"] = _krows(f(inputs["w_conv_out"][0]))
    shared["wout"] = _krows(f(inputs["w_out"][0]))
    shared["wg"] = _kchunk(f(inputs["w_gate"][0]), 128)
    shared["wu"] = _kchunk(f(inputs["w_up"][0]), 128)
    shared["wd"] = _krows(f(inputs["w_down"][0]))
    x = f(inputs["x"])
    in_maps = []
    for c in range(8):
        m = dict(shared)
        m["x"] = np.ascontiguousarray(x[c])
        in_maps.append(m)
    return in_maps


_NC_CACHE = {}


def kernel(**inputs):
    in_maps = prepare(inputs)
    if "nc" not in _NC_CACHE:
        _NC_CACHE["nc"] = build(False)
    nc = _NC_CACHE["nc"]
    res = run_bass_kernel_spmd(nc, in_maps, core_ids=list(range(8)))
    out = np.stack([np.asarray(r["out"], dtype=np.float32).reshape(S, D) for r in res.results], axis=0)
    return out
```

```python
import numpy as np
import concourse.bass as bass
import concourse.mybir as mybir
from concourse.bass_utils import run_bass_kernel_spmd

F32 = mybir.dt.float32
BF16 = mybir.dt.bfloat16
AF = mybir.ActivationFunctionType
ALU = mybir.AluOpType

S = 2048
D = 1024
NT = S // 128
DFF = 2816
NFF = DFF // 128
ALPHA = 2.0 ** 0.25
EPS = 1e-5
SBUF_BASE = 16640
SBUF_TOP = 229312

C_BGATE = 0
C_CONVW = 16
C_CONVB = 140
C_CONVG = 144
C_CONVLB = 148
C_SINK = 152
C_SMALL = 160


class _Op:
    __slots__ = ("eng", "fn", "dma", "deps", "sig", "token", "extra")

    def __init__(self, eng, fn, dma):
        self.eng = eng
        self.fn = fn
        self.dma = dma
        self.deps = {}
        self.sig = False
        self.token = None
        self.extra = None


class Sched:
    ENGS = ("pe", "act", "dve", "pool", "sp")
    NSLOT = 12

    def __init__(self):
        self.ops = {e: [] for e in self.ENGS}
        self.all = []
        self.lastw = {}
        self.readers = {}
        self.dmas_since_fence = []
        self.tens = {}

    def register(self, name, off, nbytes):
        al = []
        for n, t in self.tens.items():
            if not (t["end"] <= off or off + nbytes <= t["off"]):
                al.append(n)
                t["aliases"].append(name)
        self.tens[name] = dict(off=off, end=off + nbytes, last={}, dmas=[], aliases=al)

    def _dep(self, op, o, kind):
        if o is op:
            return
        if o.eng == op.eng and not op.dma and not o.dma:
            if op.eng == "pe":
                return
            if kind != "raw":
                return
        op.deps[o] = True

    def add(self, eng, fn, reads=(), writes=(), dma=False):
        op = _Op(eng, fn, dma)
        for r in reads:
            w = self.lastw.get(r)
            if w is not None:
                self._dep(op, w, "raw")
        for r in writes:
            w = self.lastw.get(r)
            if w is not None:
                self._dep(op, w, "waw")
            for rd in self.readers.get(r, ()):
                self._dep(op, rd, "war")
        touched = set()
        for r in list(reads) + list(writes):
            tn = r[0] if isinstance(r, tuple) else r
            if tn in self.tens:
                touched.add(tn)
        for tn in touched:
            t = self.tens[tn]
            for a in t["aliases"]:
                ta = self.tens[a]
                for o in ta["last"].values():
                    self._dep(op, o, "war")
                for o in ta["dmas"]:
                    self._dep(op, o, "war")
            if dma:
                t["dmas"].append(op)
            else:
                t["last"][eng] = op
        for r in reads:
            self.readers.setdefault(r, []).append(op)
        for r in writes:
            self.lastw[r] = op
            self.readers[r] = []
        self.ops[eng].append(op)
        self.all.append(op)
        if dma:
            self.dmas_since_fence.append(op)
        return op

    def dma(self, eng, fn, reads=(), writes=()):
        return self.add(eng, fn, reads, writes, dma=True)

    def fence(self):
        lasts = []
        for e in self.ENGS:
            for o in reversed(self.ops[e]):
                if not o.dma and o.fn is not None:
                    lasts.append(o)
                    break
        dmas = list(self.dmas_since_fence)
        self.dmas_since_fence = []
        for e in self.ENGS:
            op = _Op(e, None, False)
            for o in lasts:
                if o.eng != e:
                    op.deps[o] = True
            for o in dmas:
                op.deps[o] = True
            self.ops[e].append(op)
            self.all.append(op)

    def finalize(self, nc, stack):
        for op in self.all:
            for d in op.deps:
                d.sig = True
        self.sems = {}
        for e in ("pe", "act", "dve", "pool"):
            self.sems[e] = stack.enter_context(nc.semaphore("s_" + e))
        self.slot_sems = {}
        for q in ("sp", "pool"):
            self.slot_sems[q] = [stack.enter_context(nc.semaphore("d_%s%d" % (q, i)))
                                 for i in range(self.NSLOT)]
        for e in self.ENGS:
            cnt = 0
            nd = 0
            for op in self.ops[e]:
                if op.dma:
                    slot = nd % self.NSLOT
                    k = nd // self.NSLOT + 1
                    sem = self.slot_sems[e][slot]
                    op.token = (sem, 16 * k)
                    op.extra = (sem, 16 * (k - 1)) if k > 1 else None
                    nd += 1
                elif op.sig:
                    cnt += 1
                    op.token = (self.sems[e], cnt)

    def run(self, ename, eng):
        seen = {}
        for op in self.ops[ename]:
            waits = {}
            for d in op.deps:
                sem, val = d.token
                key = id(sem)
                if key not in waits or waits[key][1] < val:
                    waits[key] = (sem, val)
            if op.extra is not None:
                sem, val = op.extra
                key = id(sem)
                if key not in waits or waits[key][1] < val:
                    waits[key] = (sem, val)
            for key, (sem, val) in waits.items():
                if seen.get(key, 0) < val:
                    eng.wait_ge(sem, val)
                    seen[key] = val
            if op.fn is None:
                continue
            ins = op.fn(eng)
            if op.dma:
                ins.then_inc(op.token[0], 16)
            elif op.sig:
                ins.then_inc(op.token[0], 1)


def build(debug=False):
    import contextlib
    nc = bass.Bass("TRN2", target_bir_lowering=False)

    def din(name, shape):
        return nc.dram_tensor(name, shape, F32, kind="ExternalInput").ap()

    x_d = din("x", [S, D])
    lnp_d = din("lnp", [128, 6 * D])
    win_d = din("win", [30, 128, 1024])
    small_d = din("small", [128, C_SMALL])
    bias_d = din("biasT", [3, 128, 1024])
    mask_d = din("mask", [3, 128, 1024])
    ident_d = din("ident", [128, 128])
    diag_d = din("diagw", [128, 4 * 31 * 128])
    wao_d = din("wao", [128, 4 * 1024])
    wco_d = din("wco", [128, 4 * 1024])
    wout_d = din("wout", [128, 8 * 1024])
    wg_d = din("wg", [NFF, 128, 1024])
    wu_d = din("wu", [NFF, 128, 1024])
    wd_d = din("wd", [128, NFF * 1024])
    out_d = nc.dram_tensor("out", [S, D], F32, kind="ExternalOutput").ap()
    dbg = {}
    if debug:
        for nm, shp in (("d_x0T", [128, 8 * S]), ("d_qT", [64, 8 * S]), ("d_kT", [64, 2 * S]),
                        ("d_v", [128, 16 * 128]), ("d_ycin", [128, 4 * S]), ("d_oT", [64, 8 * S]),
                        ("d_mT", [128, 8 * S]), ("d_hT", [128, 8 * S])):
            dbg[nm] = nc.dram_tensor(nm, shp, F32, kind="ExternalOutput").ap()

    sc = Sched()
    names = [0]

    def sb(off, shape, dt, name):
        names[0] += 1
        h = nc.alloc_sbuf_tensor_at("%s_%d" % (name, names[0]), shape, dt, offset=off)
        nbytes = int(np.prod(shape[1:])) * (4 if dt == F32 else 2)
        assert off >= SBUF_BASE and off + nbytes <= SBUF_TOP, (name, off, nbytes)
        sc.register(name, off, nbytes)
        return h, off + ((nbytes + 31) // 32) * 32

    def sbn(off, n, shape, dt, name):
        hs = []
        for i in range(n):
            h, off = sb(off, shape, dt, "%s%d" % (name, i))
            hs.append(h)
        return hs, off

    o = SBUF_BASE
    ident, o = sb(o, [128, 128], BF16, "ident")
    ones_bf, o = sb(o, [128, 64], BF16, "ones_bf")
    onesm, o = sb(o, [128, 128], F32, "onesm")
    small, o = sb(o, [128, C_SMALL], F32, "small")
    esink, o = sb(o, [128, 8], F32, "esink")
    epsc, o = sb(o, [128, 8], F32, "epsc")
    NST = 8
    stt = []
    for i in range(NST):
        st, o = sb(o, [128, 12], F32, "st%d" % i)
        mv, o = sb(o, [128, 2], F32, "mv%d" % i)
        rs, o = sb(o, [128, 1], F32, "rs%d" % i)
        nm, o = sb(o, [128, 1], F32, "nm%d" % i)
        stt.append((st, mv, rs, nm))
    mvA, o = sb(o, [128, 32], F32, "mvA")
    rsA, o = sb(o, [128, 16], F32, "rsA")
    nmA, o = sb(o, [128, 16], F32, "nmA")
    mvE, o = sbn(o, 2, [128, 8], F32, "mvE")
    rsE, o = sbn(o, 2, [128, 4], F32, "rsE")
    nmE, o = sbn(o, 2, [128, 4], F32, "nmE")
    lnp, o = sb(o, [128, 6 * D], F32, "lnp")
    MT_OFF = o
    mT, o = sb(MT_OFF, [128, 8 * S], BF16, "mT")
    AD_OFF = o
    x0T, o = sb(o, [128, 8 * S], BF16, "x0T")
    ycin, o = sb(o, [128, 4 * S], BF16, "ycin")
    oT, o = sb(o, [128, 8 * S], BF16, "oT")
    wbuf, o = sbn(o, 4, [128, 1024], BF16, "wbuf")
    PH_OFF = o

    ps = nc.alloc_psum_tensor("ps", [128, 4096], F32)
    psb = ps.bitcast(BF16)

    def bank(b):
        return ps[:, b * 512:(b + 1) * 512]

    def bankbf(b):
        return psb[:, b * 1024:(b + 1) * 1024]

    lnv = lnp[:, :].rearrange("p (r d) -> p r d", r=6)
    ln_ctr = [0]

    def ln_stages(src, reg, gi, bi, dst, dst_reg, save=None):
        k = ln_ctr[0] % NST
        ln_ctr[0] += 1
        st, mv, rs, nm = stt[k]
        rs, nm = rs[:, :], nm[:, :]
        sk, mk, rk, nk = "st%d" % k, "mv%d" % k, "rs%d" % k, "nm%d" % k
        if save is not None:
            rs, nm, rk, nk = save

        def L0():
            def f_stats(e):
                e.bn_stats(st[:, 0:6], src[:, 0:512])
                return e.bn_stats(st[:, 6:12], src[:, 512:1024])
            sc.add("dve", f_stats, reads=[reg], writes=[sk])
            sc.add("dve", lambda e: e.bn_aggr(mv[:, :], st[:, 0:12]), reads=[sk], writes=[mk])
            sc.add("act", lambda e: e.activation(rs, mv[:, 1:2], AF.Sqrt, bias=epsc[:, 0:1]),
                   reads=[mk, "epsc"], writes=[rk])

        def L1():
            sc.add("dve", lambda e: e.reciprocal(rs, rs), reads=[rk], writes=[rk])
            sc.add("dve", lambda e: e.scalar_tensor_tensor(nm, mv[:, 0:1], -1.0, rs, ALU.mult, ALU.mult),
                   reads=[mk, rk], writes=[nk])
            sc.add("act", lambda e: e.activation(src, src, AF.Identity, bias=nm, scale=rs),
                   reads=[reg, rk, nk], writes=[reg])
            sc.add("dve", lambda e: e.tensor_tensor(src, src, lnv[:, gi, :], ALU.mult),
                   reads=[reg, "lnp", "lnp2"], writes=[reg])

        def L2():
            sc.add("dve", lambda e: e.tensor_tensor(dst, src, lnv[:, bi, :], ALU.add),
                   reads=[reg, "lnp", "lnp2"], writes=[dst_reg])
        return [L0, L1, L2]

    def cast_load(dst_ap, src_ap, writes, reads=()):
        sc.dma("pool", lambda e: e.dma_start(out=dst_ap, in_=src_ap), reads=reads, writes=writes)

    def dbg_dump(name, src_handle, nparts, ncols):
        if not debug:
            return
        sc.fence()
        for c0 in range(0, ncols, 2048):
            c1 = min(ncols, c0 + 2048)
            sc.dma("pool", lambda e, c0=c0, c1=c1: e.dma_start(out=dbg[name][0:nparts, c0:c1],
                                                              in_=src_handle[0:nparts, c0:c1]))
        sc.fence()

    def load_consts():
        sc.dma("sp", lambda e: e.dma_start(out=small[:, :], in_=small_d), writes=["small"])
        sc.dma("sp", lambda e: e.dma_start(out=lnp[:, 0:2 * D], in_=lnp_d[:, 0:2 * D]), writes=["lnp"])

    def load_consts2():
        sc.dma("sp", lambda e: e.dma_start(out=lnp[:, 2 * D:4 * D], in_=lnp_d[:, 2 * D:4 * D]), writes=["lnp2"])
        sc.dma("sp", lambda e: e.dma_start(out=lnp[:, 4 * D:6 * D], in_=lnp_d[:, 4 * D:6 * D]), writes=["lnp2"])
    cast_load(ident[:, :], ident_d, ["ident"])
    sc.add("dve", lambda e: e.memset(ones_bf[:, :], 1.0), writes=["ones_bf"])
    sc.add("dve", lambda e: e.memset(onesm[:, :], 1.0 / 512.0), writes=["onesm"])
    sc.add("dve", lambda e: e.memset(epsc[:, :], EPS), writes=["epsc"])

    o = PH_OFF
    glu, o = sb(o, [128, 4 * 2080], BF16, "glu")
    sig, o = sbn(o, 2, [128, 512], F32, "sig")
    PH2 = o
    NXA = 8
    NXB = 4
    xtA, _ = sbn(AD_OFF + 32768 + 16384, NXA, [128, D], F32, "xtA")
    x0b, _ = sbn(AD_OFF + 32768, NXB, [128, D], BF16, "x0b")
    cwx, _ = sbn(AD_OFF + 32768 + 8192, 4, [128, 1024], BF16, "cw")
    cwt = [(wbuf[i], "wbuf%d" % i) for i in range(4)] + [(cwx[i], "cw%d" % i) for i in range(4)]
    x0Tv = x0T[:, :].rearrange("p (k s) -> p k s", k=8)
    o = MT_OFF
    diag, o = sb(o, [128, 4 * 31 * 128], BF16, "diag")
    assert o <= AD_OFF
    gluv = glu[:, :].rearrange("p (c s) -> p c s", c=4)
    diagv = diag[:, :].rearrange("p (c j m) -> p c j m", c=4, j=31)
    ycv = ycin[:, :].rearrange("p (c s) -> p c s", c=4)

    def transpose_to(srcb, src_reg, dstv, dst_reg, t, pb):
        def f_tr(e):
            ins = None
            for kc in range(8):
                ins = e.transpose(bankbf(pb)[:, kc * 128:(kc + 1) * 128],
                                  srcb[:, kc * 128:(kc + 1) * 128], ident[:, :])
            return ins
        sc.add("pe", f_tr, reads=[src_reg, "ident"], writes=[("ps", pb)])
        sc.add("act", lambda e: e.copy(dstv[:, :, t * 128:(t + 1) * 128],
                                       bankbf(pb)[:, 0:1024].rearrange("p (k c) -> p k c", k=8)),
               reads=[("ps", pb)], writes=[dst_reg])

    wctr = [0]

    def load_w(chunk):
        slot = wctr[0] % 4
        wctr[0] += 1
        cast_load(wbuf[slot][:, :], win_d[chunk], ["wbuf%d" % slot])
        return slot

    def wv(slot):
        return wbuf[slot][:, :].rearrange("p (k c) -> p k c", k=8)

    def x0T_regs(tc):
        return [("x0T", 4 * tc + i) for i in range(4)]

    def proj_w(wt, wname, c0, c1, tc, pb, prows):
        def f(e):
            ins = None
            w = wt[:, :].rearrange("p (k c) -> p k c", k=8)
            for kc in range(8):
                ins = e.matmul(bank(pb)[0:prows, :], w[:, kc, c0:c1],
                               x0Tv[:, kc, tc * 512:(tc + 1) * 512],
                               start=(kc == 0), stop=(kc == 7))
            return ins
        sc.add("pe", f, reads=[wname] + x0T_regs(tc), writes=[("ps", pb)])

    def proj(slot, c0, c1, tc, pb, prows):
        proj_w(wbuf[slot], "wbuf%d" % slot, c0, c1, tc, pb, prows)

    for cc in range(4):
        sc.add("pool", lambda e, cc=cc: e.memset(gluv[:, cc, 0:15], 0.0), writes=[("glu", cc)])
        sc.add("pool", lambda e, cc=cc: e.memset(gluv[:, cc, 15 + S:30 + S], 0.0), writes=[("glu", cc)])
    diag_todo = []
    for cc in range(4):
        for jj in range(0, 31, 8):
            j1 = min(31, jj + 8)
            c0, c1 = (cc * 31 + jj) * 128, (cc * 31 + j1) * 128
            diag_todo.append(lambda cc=cc, c0=c0, c1=c1: cast_load(
                diag[:, c0:c1], diag_d[:, c0:c1], [("diag", cc)]))

    def gl_ensure(p):
        pass

    def glu_block(tc, ccs=(0, 1, 2, 3)):
        for cc in ccs:
            p = tc * 4 + cc
            i2 = p % 2
            pa, pbk = 2 * i2, 2 * i2 + 1
            proj_w(cwt[cc][0], cwt[cc][1], 0, 128, tc, pa, 128)
            proj_w(cwt[4 + cc][0], cwt[4 + cc][1], 0, 128, tc, pbk, 128)
            sg = sig[i2]
            sc.add("act", lambda e, sg=sg, pbk=pbk: e.activation(sg[:, :], bank(pbk), AF.Sigmoid),
                   reads=[("ps", pbk)], writes=["sig%d" % i2])
            sc.add("dve", lambda e, sg=sg, pa=pa, cc=cc, tc=tc: e.tensor_tensor(
                gluv[:, cc, 15 + tc * 512:15 + (tc + 1) * 512], bank(pa), sg[:, :], ALU.mult),
                reads=[("ps", pa), "sig%d" % i2], writes=[("glu", cc)])

    mvAv = mvA[:, :].rearrange("p (t c) -> p t c", c=2)
    rsAv = rsA[:, :].rearrange("p (t o) -> p t o", o=1)
    nmAv = nmA[:, :].rearrange("p (t o) -> p t o", o=1)

    def a_stage(G):
        for i in range(4):
            t = 4 * G + i
            xi = t % NXA
            xs = xtA[xi]
            xn = "xtA%d" % xi
            k = ln_ctr[0] % NST
            ln_ctr[0] += 1
            st = stt[k][0]
            sk = "st%d" % k
            sc.dma("sp", lambda e, t=t, xs=xs: e.dma_start(out=xs[:, :], in_=x_d[t * 128:(t + 1) * 128, :]),
                   writes=[xn])

            def f_stats(e, st=st, xs=xs):
                e.bn_stats(st[:, 0:6], xs[:, 0:512])
                return e.bn_stats(st[:, 6:12], xs[:, 512:1024])
            sc.add("dve", f_stats, reads=[xn], writes=[sk])
            sc.add("dve", lambda e, st=st, t=t: e.bn_aggr(mvAv[:, t, :], st[:, 0:12]),
                   reads=[sk], writes=[("mvA", t)])

    def b_stage(G):
        tl = list(range(4 * G, 4 * G + 4))
        sc.add("act", lambda e: e.activation(rsAv[:, 4 * G:4 * G + 4, :], mvAv[:, 4 * G:4 * G + 4, 1:2],
                                             AF.Sqrt, bias=epsc[:, 0:1]),
               reads=[("mvA", t) for t in tl] + ["epsc"], writes=[("rsA", t) for t in tl])
        sc.add("dve", lambda e: e.reciprocal(rsA[:, 4 * G:4 * G + 4], rsA[:, 4 * G:4 * G + 4]),
               reads=[("rsA", t) for t in tl], writes=[("rsA", t) for t in tl])
        sc.add("dve", lambda e: e.scalar_tensor_tensor(
            nmAv[:, 4 * G:4 * G + 4, :], mvAv[:, 4 * G:4 * G + 4, 0:1], -1.0, rsAv[:, 4 * G:4 * G + 4, :],
            ALU.mult, ALU.mult),
            reads=[("mvA", t) for t in tl] + [("rsA", t) for t in tl], writes=[("nmA", t) for t in tl])

    def cF_stage(G, blk=None):
        for i in range(4):
            t = 4 * G + i
            xs = xtA[t % NXA]
            xn = "xtA%d" % (t % NXA)
            sc.add("act", lambda e, xs=xs, t=t: e.activation(xs[:, :], xs[:, :], AF.Identity,
                                                             bias=nmA[:, t:t + 1], scale=rsA[:, t:t + 1]),
                   reads=[xn, ("rsA", t), ("nmA", t)], writes=[xn])
        for i in range(4):
            t = 4 * G + i
            xs = xtA[t % NXA]
            xn = "xtA%d" % (t % NXA)
            xb = x0b[t % NXB]
            bn = "x0b%d" % (t % NXB)
            sc.add("dve", lambda e, xs=xs: e.tensor_tensor(xs[:, :], xs[:, :], lnv[:, 0, :], ALU.mult),
                   reads=[xn, "lnp"], writes=[xn])
            sc.add("dve", lambda e, xs=xs, xb=xb: e.tensor_tensor(xb[:, :], xs[:, :], lnv[:, 1, :], ALU.add),
                   reads=[xn, "lnp"], writes=[bn])
            if blk is not None:
                glu_block(blk, [i])

    def cB_stage(G):
        for i in range(4):
            t = 4 * G + i
            xb = x0b[t % NXB]
            bn = "x0b%d" % (t % NXB)
            transpose_to(xb, bn, x0Tv, ("x0T", t), t, 6 + t % 2)
        if G >= 1:
            for _ in range(4):
                if diag_todo:
                    diag_todo.pop(0)()

    a_stage(0)
    load_consts()
    b_stage(0)
    a_stage(1)
    for i in range(8):
        cast_load(cwt[i][0][:, :], win_d[6 + i], [cwt[i][1]], reads=["xtA%d" % (4 + i % 4)])
    cF_stage(0)
    cB_stage(0)
    b_stage(1)
    a_stage(2)
    cF_stage(1, 0)
    cB_stage(1)
    b_stage(2)
    a_stage(3)
    cF_stage(2, 1)
    cB_stage(2)
    b_stage(3)
    cF_stage(3, 2)
    cB_stage(3)
    glu_block(3)
    load_consts2()
    while diag_todo:
        diag_todo.pop(0)()
    dbg_dump("d_x0T", x0T, 128, 8 * S)

    o = PH2
    dwb, o = sbn(o, 2, [128, 4 * 512], F32, "dwb")
    sq, o = sb(o, [128, 4 * 512], F32, "sq")
    m2b, o = sb(o, [128, 512], F32, "m2b")
    varb, o = sb(o, [128, 512], F32, "varb")
    rstdb, o = sb(o, [128, 512], F32, "rstdb")
    tbuf, o = sbn(o, 2, [128, 512], F32, "tbuf")
    for tc in range(4):
        d = tc % 2
        dv = dwb[d][:, :].rearrange("p (c s) -> p c s", c=4)
        sqv = sq[:, :].rearrange("p (c s) -> p c s", c=4)
        dn = "dwb%d" % d
        for cc in range(4):
            pb = 4 + (cc % 2)

            def f_conv(e, cc=cc, tc=tc, pb=pb):
                ins = None
                for j in range(31):
                    ins = e.matmul(bank(pb), diagv[:, cc, j, :],
                                   gluv[:, cc, tc * 512 + j:tc * 512 + j + 512],
                                   start=(j == 0), stop=(j == 30))
                return ins
            sc.add("pe", f_conv, reads=[("glu", cc), ("diag", cc)], writes=[("ps", pb)])
            sc.add("act", lambda e, cc=cc, pb=pb, dv=dv: e.activation(
                dv[:, cc, :], bank(pb), AF.Identity, bias=small[:, C_CONVB + cc:C_CONVB + cc + 1]),
                reads=[("ps", pb), "small"], writes=[(dn, cc)])
            sc.add("act", lambda e, cc=cc, dv=dv, sqv=sqv: e.activation(sqv[:, cc, :], dv[:, cc, :], AF.Square),
                   reads=[(dn, cc)], writes=[("sq", cc)])

        def f_mean(e, dv=dv):
            ins = None
            for cc in range(4):
                ins = e.matmul(bank(6), onesm[:, :], dv[:, cc, :], start=(cc == 0), stop=(cc == 3))
            return ins
        sc.add("pe", f_mean, reads=[(dn, c) for c in range(4)] + ["onesm"], writes=[("ps", 6)])

        def f_ex2(e, sqv=sqv):
            ins = None
            for cc in range(4):
                ins = e.matmul(bank(7), onesm[:, :], sqv[:, cc, :], start=(cc == 0), stop=(cc == 3))
            return ins
        sc.add("pe", f_ex2, reads=[("sq", c) for c in range(4)] + ["onesm"], writes=[("ps", 7)])
        sc.add("act", lambda e: e.activation(m2b[:, :], bank(6), AF.Square), reads=[("ps", 6)], writes=["m2b"])
        sc.add("dve", lambda e: e.tensor_tensor(varb[:, :], bank(7), m2b[:, :], ALU.subtract),
               reads=[("ps", 7), "m2b"], writes=["varb"])
        sc.add("act", lambda e: e.activation(rstdb[:, :], varb[:, :], AF.Sqrt, bias=epsc[:, 0:1]),
               reads=["varb", "epsc"], writes=["rstdb"])
        sc.add("dve", lambda e: e.reciprocal(rstdb[:, :], rstdb[:, :]), reads=["rstdb"], writes=["rstdb"])
        for cc in range(4):
            tb = tbuf[cc % 2]
            tn = "tbuf%d" % (cc % 2)
            sc.add("dve", lambda e, cc=cc, tb=tb, dv=dv: e.tensor_tensor(tb[:, :], dv[:, cc, :], bank(6), ALU.subtract),
                   reads=[(dn, cc), ("ps", 6)], writes=[tn])
            sc.add("dve", lambda e, tb=tb: e.tensor_tensor(tb[:, :], tb[:, :], rstdb[:, :], ALU.mult),
                   reads=[tn, "rstdb"], writes=[tn])
            sc.add("act", lambda e, cc=cc, tc=tc, tb=tb: e.activation(
                ycv[:, cc, tc * 512:(tc + 1) * 512], tb[:, :], AF.Silu,
                bias=small[:, C_CONVLB + cc:C_CONVLB + cc + 1], scale=small[:, C_CONVG + cc:C_CONVG + cc + 1]),
                reads=[tn, "small"], writes=[("ycin", cc)])
    dbg_dump("d_ycin", ycin, 128, 4 * S)

    o = MT_OFF
    qT, o = sb(o, [128, 8 * S], BF16, "qT")
    assert o <= AD_OFF
    o = PH_OFF
    kT, o = sb(o, [128, 2 * S], BF16, "kT")
    vsb, o = sb(o, [128, 16 * 128], BF16, "v")
    expb, o = sb(o, [128, 3 * 1024], F32, "expb")
    bstage, o = sb(o, [128, 1024], F32, "bstage")
    mstage, o = sb(o, [128, 1024], F32, "mstage")
    esf, o = sb(o, [128, 1024], F32, "esf")
    NE = 3
    Eb, o = sbn(o, NE, [128, 512], F32, "E")
    NP = 9
    pTb, o = sbn(o, NP, [128, 512], BF16, "pT")
    trb, o = sbn(o, 2, [128, 512], F32, "tr")
    esh, o = sb(o, [128, 1024], BF16, "esh")
    esl, o = sb(o, [128, 1024], BF16, "esl")
    qTv = qT[:, :].rearrange("p (h s) -> p h s", h=8)
    qT4 = qT[:, :].rearrange("p (a b s) -> p a b s", a=2, b=4)
    oT5 = oT[:, :].rearrange("p (g i a s) -> p g a i s", g=2, i=2, a=2)
    oTce = oT[:, :].rearrange("p (c e s) -> p c e s", c=4, e=2)
    kTv = kT[:, :].rearrange("p (g s) -> p g s", g=2)
    vv = vsb[:, :].rearrange("p (t c) -> p t c", t=16)
    oTv = oT[:, :].rearrange("p (h s) -> p h s", h=8)
    esfv = esf[:, :].rearrange("p (h q) -> p h q", h=8)
    expbv = expb[:, :].rearrange("p (j c) -> p j c", j=3)

    sc.add("act", lambda e: e.activation(esink[0:64, :], small[0:64, C_SINK:C_SINK + 8], AF.Exp),
           reads=["small"], writes=["esink"])
    sc.add("pool", lambda e: e.memset(esf[0:64, :], 0.0), writes=["esf"])
    for h in range(8):
        sc.add("dve", lambda e, h=h: e.tensor_scalar(esfv[0:64, h, :], esfv[0:64, h, :], esink[0:64, h:h + 1],
                                                      None, ALU.add), reads=["esf", "esink"], writes=["esf"])
    sc.add("dve", lambda e: e.tensor_copy(esh[0:1, :], esf[0:1, :]), reads=["esf"], writes=["esh"])
    sc.add("dve", lambda e: e.tensor_tensor(esf[0:1, :], esf[0:1, :], esh[0:1, :], ALU.subtract),
           reads=["esf", "esh"], writes=["esf"])
    sc.add("dve", lambda e: e.tensor_copy(esl[0:1, :], esf[0:1, :]), reads=["esf"], writes=["esl"])
    for j in range(3):
        sc.dma("sp", lambda e, j=j: e.dma_start(out=bstage[:, :], in_=bias_d[j]), writes=["bstage"])
        sc.dma("sp", lambda e, j=j: e.dma_start(out=mstage[:, :], in_=mask_d[j]), writes=["mstage"])
        sc.add("act", lambda e: e.activation(bstage[:, :], bstage[:, :], AF.Exp), reads=["bstage"], writes=["bstage"])
        sc.add("dve", lambda e, j=j: e.tensor_tensor(expbv[:, j, :], bstage[:, :], mstage[:, :], ALU.mult),
               reads=["bstage", "mstage"], writes=["expb"])

    pctr = [0]

    def nextbank(lo, n):
        b = lo + pctr[0] % n
        pctr[0] += 1
        return b

    wslots = [load_w(c) for c in range(4)]
    sk_ = None
    for i in range(4):
        s_ = wslots[i]
        if i == 1:
            sk_ = load_w(4)
        if i == 2:
            sv_ = load_w(5)
        for tc in range(4):
            pb = nextbank(0, 4)
            proj(s_, 0, 128, tc, pb, 128)
            sc.add("act", lambda e, i=i, tc=tc, pb=pb: e.copy(qTv[:, i, tc * 512:(tc + 1) * 512], bank(pb)),
                   reads=[("ps", pb)], writes=[("qT", i)])
        sc.dma("sp", lambda e, i=i: e.dma_start(out=qTv[0:64, 4 + i, :], in_=qTv[64:128, i, :]),
               reads=[("qT", i)], writes=[("qT", 4 + i)])
    for tc in range(4):
        pb = nextbank(0, 4)
        proj(sk_, 0, 128, tc, pb, 128)
        sc.add("act", lambda e, tc=tc, pb=pb: e.copy(kTv[:, 0, tc * 512:(tc + 1) * 512], bank(pb)),
               reads=[("ps", pb)], writes=[("kT", 0)])
    sc.dma("sp", lambda e: e.dma_start(out=kTv[0:64, 1, :], in_=kTv[64:128, 0, :]),
           reads=[("kT", 0)], writes=[("kT", 1)])
    for tq in range(4):
        pb = nextbank(0, 4)

        def f_v(e, tq=tq, pb=pb):
            ins = None
            w = wv(sv_)
            for i in range(4):
                t = 4 * tq + i
                for kc in range(8):
                    ins = e.matmul(bank(pb)[:, i * 128:(i + 1) * 128], x0Tv[:, kc, t * 128:(t + 1) * 128],
                                   w[:, kc, :], start=(kc == 0), stop=(kc == 7))
            return ins
        sc.add("pe", f_v, reads=["wbuf%d" % sv_] + x0T_regs(tq), writes=[("ps", pb)])
        sc.add("act", lambda e, tq=tq, pb=pb: e.copy(
            vv[:, 4 * tq:4 * tq + 4, :], bank(pb).rearrange("p (t c) -> p t c", t=4)),
            reads=[("ps", pb)], writes=[("v", tq)])

    ectr = [0]
    pctr2 = [0]

    def att_front(n, g):
        jl = [j for j in range(3) if 0 <= n + j - 1 < 16]
        pts = []
        for j in jl:
            kb = n + j - 1
            pb = nextbank(0, 4)
            sc.add("pe", lambda e, g=g, kb=kb, n=n, pb=pb: e.matmul(
                bank(pb).rearrange("p (a b q) -> p a b q", a=2, b=2),
                kTv[0:64, g, kb * 128:(kb + 1) * 128],
                qT4[0:64, :, 2 * g:2 * g + 2, n * 128:(n + 1) * 128], start=True, stop=True),
                reads=[("kT", g)] + [("qT", sl) for sl in (2 * g, 2 * g + 1, 4 + 2 * g, 5 + 2 * g)],
                writes=[("ps", pb)])
            ei = ectr[0] % NE
            ectr[0] += 1
            Et = Eb[ei]
            sc.add("act", lambda e, Et=Et, pb=pb: e.activation(Et[:, :], bank(pb), AF.Exp, scale=0.125),
                   reads=[("ps", pb)], writes=["E%d" % ei])
            pi = pctr2[0] % NP
            pctr2[0] += 1
            pt = pTb[pi]
            mul_eng = "pool" if (pctr2[0] % 3 == 2) else "dve"
            sc.add(mul_eng, lambda e, Et=Et, pt=pt, j=j, g=g: e.tensor_tensor(
                pt[:, :], Et[:, :], expbv[:, j, g * 512:(g + 1) * 512], ALU.mult),
                reads=["E%d" % ei, "expb"], writes=["pT%d" % pi])
            pts.append((pi, pt, kb))
        return pts

    def att_back(n, g, pts):
        u = n * 2 + g
        ob = 4 + u % 2
        db = 6 + u % 2

        def f_pv(e):
            ins = None
            for idx, (pi, pt, kb) in enumerate(pts):
                ins = e.matmul(bank(ob)[0:64, :], vv[:, kb, g * 64:(g + 1) * 64], pt[:, :],
                               start=(idx == 0), stop=(idx == len(pts) - 1))
            return ins
        sc.add("pe", f_pv, reads=["pT%d" % p[0] for p in pts] + [("v", p[2] // 4) for p in pts],
               writes=[("ps", ob)])

        def f_den(e):
            for idx, (pi, pt, kb) in enumerate(pts):
                e.matmul(bank(db)[0:64, :], ones_bf[:, 0:64], pt[:, :], start=(idx == 0), stop=False)
            e.matmul(bank(db)[0:64, :], ones_bf[0:1, 0:64], esh[0:1, g * 512:(g + 1) * 512],
                     start=False, stop=False)
            return e.matmul(bank(db)[0:64, :], ones_bf[0:1, 0:64], esl[0:1, g * 512:(g + 1) * 512],
                            start=False, stop=True)
        sc.add("pe", f_den, reads=["pT%d" % p[0] for p in pts] + ["ones_bf", "esh", "esl"], writes=[("ps", db)])
        ti = u % 2
        tr = trb[ti]
        tn = "tr%d" % ti
        sc.add("act", lambda e: e.activation(tr[0:64, :], bank(db)[0:64, :], AF.Ln),
               reads=[("ps", db)], writes=[tn])
        sc.add("act", lambda e: e.activation(tr[0:64, :], tr[0:64, :], AF.Exp, scale=-1.0),
               reads=[tn], writes=[tn])
        sc.add("dve", lambda e: e.tensor_tensor(
            oT5[0:64, g, :, :, n * 128:(n + 1) * 128],
            bank(ob)[0:64, :].rearrange("p (a i q) -> p a i q", a=2, i=2),
            tr[0:64, :].rearrange("p (a i q) -> p a i q", a=2, i=2), ALU.mult),
            reads=[("ps", ob), tn], writes=[("oT", n)])
        if g == 1 and n % 4 == 3:
            tc = n // 4
            sc.dma("sp", lambda e, tc=tc: e.dma_start(out=oTce[64:128, :, 0, tc * 512:(tc + 1) * 512],
                                                       in_=oTce[0:64, :, 1, tc * 512:(tc + 1) * 512]),
                   reads=[("oT", 4 * tc + i) for i in range(4)], writes=[("oT2", tc)])

    units = [(n, g) for n in range(16) for g in range(2)]
    fr = {0: att_front(*units[0]), 1: att_front(*units[1])}
    for i, (n, g) in enumerate(units):
        if i + 2 < len(units):
            fr[i + 2] = att_front(*units[i + 2])
        att_back(n, g, fr.pop(i))
    dbg_dump("d_qT", qT, 64, 8 * S)
    dbg_dump("d_kT", kT, 64, 2 * S)
    dbg_dump("d_v", vsb, 128, 16 * 128)
    dbg_dump("d_oT", oT, 64, 8 * S)

    o = PH_OFF
    wao, o = sb(o, [128, 4 * 1024], BF16, "wao")
    wco, o = sb(o, [128, 4 * 1024], BF16, "wco")
    gab, o = sbn(o, 4, [128, 512], F32, "gab")
    t12, o = sbn(o, 4, [128, 512], F32, "t12")
    waov = wao[:, :].rearrange("p (k c) -> p k c", k=4)
    wcov = wco[:, :].rearrange("p (k c) -> p k c", k=4)
    mTv = mT[:, :].rearrange("p (k s) -> p k s", k=8)
    wout, _ = sb(SBUF_TOP - 16384, [128, 8 * 1024], BF16, "wout")
    woutv = wout[:, :].rearrange("p (k c) -> p k c", k=8)
    assert o <= SBUF_TOP - 16384
    gslots = {0: (load_w(14), load_w(22))}
    for i in range(2):
        cast_load(wao[:, i * 2048:(i + 1) * 2048], wao_d[:, i * 2048:(i + 1) * 2048], ["wao"])
    for i in range(2):
        cast_load(wco[:, i * 2048:(i + 1) * 2048], wco_d[:, i * 2048:(i + 1) * 2048], ["wco"])
    for f in range(8):
        if f + 1 < 8:
            gslots[f + 1] = (load_w(14 + f + 1), load_w(22 + f + 1))
        if 2 <= f <= 5:
            i = f - 2
            cast_load(wout[:, i * 2048:(i + 1) * 2048], wout_d[:, i * 2048:(i + 1) * 2048], ["wout"])
        sga, sgc = gslots[f]
        for tc in range(4):
            i2 = (f * 4 + tc) % 2
            b0, b1, b2, b3 = 4 * i2, 4 * i2 + 1, 4 * i2 + 2, 4 * i2 + 3
            proj(sga, 0, 128, tc, b0, 128)
            proj(sgc, 0, 128, tc, b1, 128)

            def f_ya(e, f=f, tc=tc, b2=b2):
                ins = None
                for c in range(4):
                    ins = e.matmul(bank(b2), waov[:, c, f * 128:(f + 1) * 128],
                                   oTv[:, 2 * c, tc * 512:(tc + 1) * 512], start=(c == 0), stop=(c == 3))
                return ins
            sc.add("pe", f_ya, reads=["wao", ("oT2", tc)] + [("oT", 4 * tc + i) for i in range(4)],
                   writes=[("ps", b2)])

            def f_yc(e, f=f, tc=tc, b3=b3):
                ins = None
                for c in range(4):
                    ins = e.matmul(bank(b3), wcov[:, c, f * 128:(f + 1) * 128],
                                   ycv[:, c, tc * 512:(tc + 1) * 512], start=(c == 0), stop=(c == 3))
                return ins
            sc.add("pe", f_yc, reads=["wco"] + [("ycin", c) for c in range(4)], writes=[("ps", b3)])
            ga, gc = gab[2 * i2], gab[2 * i2 + 1]
            t1, t2 = t12[2 * i2], t12[2 * i2 + 1]
            gan, gcn = "gab%d" % (2 * i2), "gab%d" % (2 * i2 + 1)
            t1n, t2n = "t12%d" % (2 * i2), "t12%d" % (2 * i2 + 1)
            sc.add("act", lambda e, ga=ga, b0=b0, f=f: e.activation(
                ga[:, :], bank(b0), AF.Sigmoid, bias=small[:, C_BGATE + f:C_BGATE + f + 1]),
                reads=[("ps", b0), "small"], writes=[gan])
            sc.add("act", lambda e, gc=gc, b1=b1, f=f: e.activation(
                gc[:, :], bank(b1), AF.Sigmoid, bias=small[:, C_BGATE + 8 + f:C_BGATE + 8 + f + 1]),
                reads=[("ps", b1), "small"], writes=[gcn])
            sc.add("dve", lambda e, t1=t1, ga=ga, b2=b2: e.tensor_tensor(t1[:, :], bank(b2), ga[:, :], ALU.mult),
                   reads=[("ps", b2), gan], writes=[t1n])
            sc.add("dve", lambda e, t2=t2, gc=gc, b3=b3: e.tensor_tensor(t2[:, :], bank(b3), gc[:, :], ALU.mult),
                   reads=[("ps", b3), gcn], writes=[t2n])
            sc.add("pool", lambda e, t1=t1, t2=t2, f=f, tc=tc: e.tensor_tensor(
                mTv[:, f, tc * 512:(tc + 1) * 512], t1[:, :], t2[:, :], ALU.add),
                reads=[t1n, t2n], writes=[("mT", 4 * tc + i) for i in range(4)])
    dbg_dump("d_mT", mT, 128, 8 * S)

    o = AD_OFF
    wdn, o = sb(o, [128, NFF * 1024], BF16, "wdn")
    hid, o = sb(o, [128, NFF * 512], BF16, "hid")
    NXE = 8
    xtE, o = sbn(o, NXE, [128, D], F32, "xtE")
    hbb, o = sbn(o, 2, [128, D], BF16, "hbb")
    NZ = 3
    zb, o = sbn(o, NZ, [128, D], F32, "zb")
    sgb, o = sbn(o, 2, [128, 512], F32, "sgb")
    NWE = 6
    wE, o = sbn(o, NWE, [128, 1024], BF16, "wE")
    assert o <= SBUF_TOP - 16384
    wdnv = wdn[:, :].rearrange("p (k c) -> p k c", k=NFF)
    hidv = hid[:, :].rearrange("p (k s) -> p k s", k=NFF)

    ectr2 = [0]
    wd_loaded = [0]

    gu_calls = [0]

    def load_gu(ffc):
        gu_calls[0] += 1
        s0 = ectr2[0] % NWE
        s1 = (ectr2[0] + 1) % NWE
        ectr2[0] += 2
        cast_load(wE[s0][:, :], wg_d[ffc], ["wE%d" % s0])
        cast_load(wE[s1][:, :], wu_d[ffc], ["wE%d" % s1])
        if wd_loaded[0] < 11 and gu_calls[0] % 2 == 0:
            i = wd_loaded[0]
            wd_loaded[0] += 1
            cast_load(wdn[:, i * 2048:(i + 1) * 2048], wd_d[:, i * 2048:(i + 1) * 2048], [("wdn", i)])
        return (s0, s1)

    def s5_schedule(ST):
        sched = {}
        mul_eng = "dve"

        def at(step, fn):
            sched.setdefault(step, []).append(fn)
        bi_ = ST % 2
        mv4, rs4, nm4 = mvE[bi_], rsE[bi_], nmE[bi_]
        mvn, rsn, nmn = "mvE%d" % bi_, "rsE%d" % bi_, "nmE%d" % bi_
        mv4v = mv4[:, :].rearrange("p (i t) -> p i t", t=2)
        rs4v = rs4[:, :].rearrange("p (i o) -> p i o", o=1)
        nm4v = nm4[:, :].rearrange("p (i o) -> p i o", o=1)
        for tl in range(4):
            t = 4 * ST + tl
            xi = t % NXE
            xs = xtE[xi]
            xn = "xtE%d" % xi
            hi = t % 2
            hb = hbb[hi]
            hn = "hbb%d" % hi
            k = ln_ctr[0] % NST
            ln_ctr[0] += 1
            st = stt[k][0]
            sk = "st%d" % k

            def s0(t=t, xs=xs, xn=xn):
                sc.dma("sp", lambda e: e.dma_start(out=xs[:, :], in_=x_d[t * 128:(t + 1) * 128, :]), writes=[xn])

            def s1(t=t, xs=xs, xn=xn):
                sc.add("act", lambda e: e.activation(xs[:, :], xs[:, :], AF.Identity,
                                                     bias=nmA[:, t:t + 1], scale=rsA[:, t:t + 1]),
                       reads=[xn, ("rsA", t), ("nmA", t)], writes=[xn])
                sc.add(mul_eng, lambda e: e.tensor_tensor(xs[:, :], xs[:, :], lnv[:, 0, :], ALU.mult),
                       reads=[xn, "lnp"], writes=[xn])
                for half in range(2):
                    pb = half

                    def f_mix(e, half=half, pb=pb):
                        ins = None
                        for kc in range(8):
                            ins = e.matmul(bank(pb), mTv[:, kc, t * 128:(t + 1) * 128],
                                           woutv[:, kc, half * 512:(half + 1) * 512],
                                           start=(kc == 0), stop=(kc == 7))
                        return ins
                    sc.add("pe", f_mix, reads=[("mT", t), "wout"], writes=[("ps", pb)])

            def s2(t=t, tl=tl, xs=xs, xn=xn, st=st, sk=sk):
                sc.add("dve", lambda e: e.tensor_tensor(xs[:, :], xs[:, :], lnv[:, 1, :], ALU.add),
                       reads=[xn, "lnp"], writes=[xn])
                for half in range(2):
                    pb = half
                    sc.add("dve", lambda e, half=half, pb=pb: e.scalar_tensor_tensor(
                        xs[:, half * 512:(half + 1) * 512], xs[:, half * 512:(half + 1) * 512], ALPHA,
                        bank(pb), ALU.mult, ALU.add), reads=[xn, ("ps", pb)], writes=[xn])

            def s2b(tl=tl, xs=xs, xn=xn, st=st, sk=sk):
                def f_stats(e):
                    e.bn_stats(st[:, 0:6], xs[:, 0:512])
                    return e.bn_stats(st[:, 6:12], xs[:, 512:1024])
                sc.add("dve", f_stats, reads=[xn], writes=[sk])
                sc.add("dve", lambda e: e.bn_aggr(mv4v[:, tl, :], st[:, 0:12]), reads=[sk], writes=[(mvn, tl)])
            at(2 * tl, s0)
            at(2 * tl + 1, s1)
            at(2 * tl + 2, s2)
            at(2 * tl + 3, s2b)

        def s_sqrt():
            sc.add("act", lambda e: e.activation(rs4v, mv4v[:, :, 1:2], AF.Sqrt, bias=epsc[:, 0:1]),
                   reads=[(mvn, i) for i in range(4)] + ["epsc"], writes=[rsn])

        def s_rn():
            sc.add("dve", lambda e: e.reciprocal(rs4[:, :], rs4[:, :]), reads=[rsn], writes=[rsn])
            sc.add("dve", lambda e: e.scalar_tensor_tensor(nm4v, mv4v[:, :, 0:1], -1.0, rs4v, ALU.mult, ALU.mult),
                   reads=[(mvn, i) for i in range(4)] + [rsn], writes=[nmn])
        at(10, s_sqrt)
        at(11, s_rn)
        for tl in range(4):
            t = 4 * ST + tl
            xi = t % NXE
            xs = xtE[xi]
            xn = "xtE%d" % xi
            hi = t % 2
            hb = hbb[hi]
            hn = "hbb%d" % hi

            def s3(tl=tl, xs=xs, xn=xn):
                sc.add("act", lambda e: e.activation(xs[:, :], xs[:, :], AF.Identity,
                                                     bias=nm4[:, tl:tl + 1], scale=rs4[:, tl:tl + 1]),
                       reads=[xn, rsn, nmn], writes=[xn])
                sc.add(mul_eng, lambda e: e.tensor_tensor(xs[:, :], xs[:, :], lnv[:, 2, :], ALU.mult),
                       reads=[xn, "lnp2"], writes=[xn])

            def s4(xs=xs, xn=xn, hb=hb, hn=hn):
                sc.add("dve", lambda e: e.tensor_tensor(xs[:, :], xs[:, :], lnv[:, 3, :], ALU.add),
                       reads=[xn, "lnp2"], writes=[xn])
                sc.add("act", lambda e: e.copy(hb[:, :], xs[:, :]), reads=[xn], writes=[hn])

            def s5(t=t, hb=hb, hn=hn, hi=hi):
                transpose_to(hb, hn, mTv, ("mT", t), t, 2 + hi)
            at(12 + tl, s3)
            at(13 + tl, s4)
            at(14 + tl, s5)
        return sched

    sched0 = s5_schedule(0)
    for step in sorted(sched0):
        for fn in sched0[step]:
            fn()

    zctr = [0]
    dn_deferred = []
    pend = {}
    gu_ctr = [0]

    def prefetch_gu(gidx):
        if gidx < 4 * NFF and gidx not in pend:
            pend[gidx] = load_gu(gidx % NFF)

    prefetch_gu(0)
    prefetch_gu(1)
    for ST in range(4):
        bgs = s5_schedule(ST + 1) if ST + 1 < 4 else {}
        for ffc in range(NFF):
            gidx = ST * NFF + ffc
            prefetch_gu(gidx + 2)
            s0, s1 = pend.pop(gidx)
            i2 = ffc % 2
            bg, bu = 4 + 2 * i2, 5 + 2 * i2
            for (slot, pb) in ((s0, bg), (s1, bu)):
                def f_gu(e, slot=slot, pb=pb, ST=ST):
                    ins = None
                    w = wE[slot][:, :].rearrange("p (k c) -> p k c", k=8)
                    for kc in range(8):
                        ins = e.matmul(bank(pb), w[:, kc, :], mTv[:, kc, ST * 512:(ST + 1) * 512],
                                       start=(kc == 0), stop=(kc == 7))
                    return ins
                sc.add("pe", f_gu, reads=["wE%d" % slot] + [("mT", 4 * ST + i) for i in range(4)],
                       writes=[("ps", pb)])
            sg = sgb[i2]
            sc.add("act", lambda e, sg=sg, bg=bg: e.activation(sg[:, :], bank(bg), AF.Silu),
                   reads=[("ps", bg)], writes=["sgb%d" % i2])
            sc.add("dve", lambda e, sg=sg, bu=bu, ffc=ffc: e.tensor_tensor(
                hidv[:, ffc, :], bank(bu), sg[:, :], ALU.mult),
                reads=[("ps", bu), "sgb%d" % i2], writes=[("hid", ffc)])
            if ffc == 1:
                while dn_deferred:
                    dn_deferred.pop(0)()
            step = ffc - 2
            for fn in bgs.get(step, ()):
                fn()
        zst = {}
        for it in range(4 + 1):
            if it < 4:
                tl = it
                t = 4 * ST + tl
                zi = zctr[0] % NZ
                zctr[0] += 1
                z = zb[zi]
                zn = "zb%d" % zi
                hs = xtE[t % NXE]
                hsn = "xtE%d" % (t % NXE)
                for half in range(2):
                    pb = 2 * (tl % 2) + half

                    def f_dn(e, tl=tl, half=half, pb=pb):
                        ins = None
                        for k in range(NFF):
                            ins = e.matmul(bank(pb), hidv[:, k, tl * 128:(tl + 1) * 128],
                                           wdnv[:, k, half * 512:(half + 1) * 512],
                                           start=(k == 0), stop=(k == NFF - 1))
                        return ins
                    sc.add("pe", f_dn, reads=[("hid", k) for k in range(NFF)] + [("wdn", i) for i in range(11)],
                           writes=[("ps", pb)])
                    sc.add("dve", lambda e, z=z, hs=hs, half=half, pb=pb: e.scalar_tensor_tensor(
                        z[:, half * 512:(half + 1) * 512], hs[:, half * 512:(half + 1) * 512], ALPHA,
                        bank(pb), ALU.mult, ALU.add), reads=[hsn, ("ps", pb)], writes=[zn])
                zst[tl] = (ln_stages(z[:, :], zn, 4, 5, z[:, :], zn), z, zn, t)
                zst[tl][0][0]()
            if 0 <= it - 1 < 4:
                def fin(tup=zst[it - 1]):
                    lst, z, zn, t = tup
                    lst[1]()
                    lst[2]()
                    sc.dma("sp", lambda e: e.dma_start(out=out_d[t * 128:(t + 1) * 128, :], in_=z[:, :]),
                           reads=[zn])
                if it - 1 == 3 and ST < 3:
                    dn_deferred.append(fin)
                else:
                    fin()
    sc.fence()
    if debug:
        dbg_dump("d_hT", mT, 128, 8 * S)

    with contextlib.ExitStack() as stack:
        sc.finalize(nc, stack)
        with nc.Block() as block:
            @block.tensor
            def _(e):
                sc.run("pe", e)

            @block.scalar
            def _(e):
                sc.run("act", e)

            @block.vector
            def _(e):
                sc.run("dve", e)

            @block.gpsimd
            def _(e):
                sc.run("pool", e)

            @block.sync
            def _(e):
                sc.run("sp", e)
    return nc


NUM_BUCKETS = 32
HEAD_PERM = [0, 2, 1, 3, 4, 6, 5, 7]
MAX_DISTANCE = 128


def _t5_buckets(rel):
    nb = NUM_BUCKETS // 2
    ret = (rel > 0).astype(np.int32) * nb
    n = np.abs(rel)
    max_exact = nb // 2
    large = max_exact + (np.log(np.maximum(n, 1) / max_exact)
                         / np.log(MAX_DISTANCE / max_exact) * (nb - max_exact)).astype(np.int32)
    large = np.minimum(large, nb - 1)
    return (ret + np.where(n < max_exact, n, large)).astype(np.int32)


def _kchunk(w, ncols_chunk):
    K, C = w.shape
    nk = K // 128
    nch = C // ncols_chunk
    a = w.reshape(nk, 128, nch, ncols_chunk).transpose(2, 1, 0, 3)
    return np.ascontiguousarray(a.reshape(nch, 128, nk * ncols_chunk))


def _krows(w):
    K, C = w.shape
    nk = K // 128
    return np.ascontiguousarray(w.reshape(nk, 128, C).transpose(1, 0, 2).reshape(128, nk * C))


def prepare(inputs):
    f = lambda a: np.asarray(a, dtype=np.float32)
    shared = {}
    rep = lambda v: np.broadcast_to(f(v).reshape(1, -1), (128, f(v).size))
    shared["lnp"] = np.ascontiguousarray(np.concatenate(
        [rep(inputs["ln_in_g"]), rep(inputs["ln_in_b"]), rep(inputs["ln1_g"][0]), rep(inputs["ln1_b"][0]),
         rep(inputs["ln2_g"][0]), rep(inputs["ln2_b"][0])], axis=1))
    shared["win"] = _kchunk(f(inputs["w_in"][0]), 128)
    small = np.zeros((128, C_SMALL), np.float32)
    small[:, C_BGATE:C_BGATE + 16] = f(inputs["b_gate"][0]).reshape(16, 128).T
    cw = f(inputs["conv_w"][0])
    small[:, C_CONVW:C_CONVW + 124] = cw.T.reshape(4, 128, 31).transpose(1, 0, 2).reshape(128, 124)
    small[:, C_CONVB:C_CONVB + 4] = f(inputs["conv_b"][0]).reshape(4, 128).T
    small[:, C_CONVG:C_CONVG + 4] = f(inputs["conv_ln_g"][0]).reshape(4, 128).T
    small[:, C_CONVLB:C_CONVLB + 4] = f(inputs["conv_ln_b"][0]).reshape(4, 128).T
    small[:, C_SINK:C_SINK + 8] = f(inputs["sink"][0])[HEAD_PERM].reshape(1, 8)
    shared["small"] = small
    dg = np.zeros((128, 4, 31, 128), np.float32)
    pidx = np.arange(128)
    dg[pidx, :, :, pidx] = cw.T.reshape(4, 128, 31).transpose(1, 0, 2)
    shared["diagw"] = np.ascontiguousarray(dg.reshape(128, 4 * 31 * 128))
    k = np.arange(128)[:, None]
    q = np.arange(128)[None, :]
    rb = f(inputs["rel_bias"])
    biasT = np.zeros((3, 128, 8, 128), np.float32)
    mask = np.zeros((3, 128, 8, 128), np.float32)
    for j in range(3):
        rel = j * 128 + k - 128 - q
        bk = _t5_buckets(rel)
        biasT[j] = rb[bk].transpose(0, 2, 1)[:, HEAD_PERM, :]
        mask[j] = (np.abs(rel) <= 128).astype(np.float32)[:, None, :]
    shared["biasT"] = np.ascontiguousarray(biasT.reshape(3, 128, 1024))
    shared["mask"] = np.ascontiguousarray(mask.reshape(3, 128, 1024))
    shared["ident"] = np.eye(128, dtype=np.float32)
    shared["wao"] = _krows(f(inputs["w_attn_out"][0]))
    shared["wco"] = _krows(f(inputs["w_conv_out"][0]))
    shared["wout"] = _krows(f(inputs["w_out"][0]))
    shared["wg"] = _kchunk(f(inputs["w_gate"][0]), 128)
    shared["wu"] = _kchunk(f(inputs["w_up"][0]), 128)
    shared["wd"] = _krows(f(inputs["w_down"][0]))
    x = f(inputs["x"])
    in_maps = []
    for c in range(8):
        m = dict(shared)
        m["x"] = np.ascontiguousarray(x[c])
        in_maps.append(m)
    return in_maps


_NC_CACHE = {}


def kernel(**inputs):
    in_maps = prepare(inputs)
    if "nc" not in _NC_CACHE:
        _NC_CACHE["nc"] = build(False)
    nc = _NC_CACHE["nc"]
    res = run_bass_kernel_spmd(nc, in_maps, core_ids=list(range(8)))
    out = np.stack([np.asarray(r["out"], dtype=np.float32).reshape(S, D) for r in res.results], axis=0)
    return out
```
